# Optimizing a Trainium2 kernel written in Bass

```python
import math
import jax, jax.numpy as jnp
from jax import lax
import numpy as np


D_MODEL = 1024
BATCH = 8
SEQ = 2048
DEPTH = 1
DEC_BATCH = 128
DEC_SEQ = 4
PAST_LEN = 16384
PAGE_SIZE = 128

M_HEADS = 4
M_DK = D_MODEL // 8
M_DV = D_MODEL // 8
M_WIDTH = M_HEADS * M_DV
M_CHUNK = 128
A_HEADS = 8
A_KV_HEADS = 2
A_GROUP = A_HEADS // A_KV_HEADS
A_HEAD_DIM = 64
A_WIDTH = A_HEADS * A_HEAD_DIM
A_KV_WIDTH = A_KV_HEADS * A_HEAD_DIM
WINDOW = 128
ROT_DIM = A_HEAD_DIM // 4
ROPE_THETA = 500000.0
NORM_EPS = 1e-6
IN_SIZES = (M_HEADS * M_DK, M_HEADS * M_DK, M_WIDTH, M_HEADS, M_HEADS, M_WIDTH, M_WIDTH,
            A_WIDTH, A_KV_WIDTH, A_KV_WIDTH, A_WIDTH, D_MODEL, D_MODEL)
IN_COLS = sum(IN_SIZES)

kernel_name = 'hybrid_mlstm_swa_sink_step'


def rmsnorm(x, g):
    xf = x.astype(jnp.float32)
    y = xf * lax.rsqrt(jnp.mean(xf * xf, axis=-1, keepdims=True) + NORM_EPS)
    return (y * g.astype(jnp.float32)).astype(x.dtype)


def split_in(z):
    out, start = [], 0
    for size in IN_SIZES:
        out.append(z[..., start:start + size])
        start += size
    return out


def adaln_in(x, c, w_ada, b_ada, g_norm):
    mod = jax.nn.silu(c) @ w_ada + b_ada
    shift, scale, gate = jnp.split(mod, 3, axis=-1)
    h = rmsnorm(x, g_norm) * (1 + scale[:, None, :]) + shift[:, None, :]
    return h, gate[:, None, :]


def partial_rope(x, pos):
    half = ROT_DIM // 2
    inv = ROPE_THETA ** (-jnp.arange(0, ROT_DIM, 2, dtype=jnp.float32) / ROT_DIM)
    ang = pos.astype(jnp.float32)[:, None] * inv
    cos, sin = jnp.cos(ang)[:, None, :], jnp.sin(ang)[:, None, :]
    xf = x.astype(jnp.float32)
    x1, x2 = xf[..., :half], xf[..., half:ROT_DIM]
    out = jnp.concatenate([x1 * cos - x2 * sin, x2 * cos + x1 * sin, xf[..., ROT_DIM:]], axis=-1)
    return out.astype(x.dtype)


def mlstm_chunk(carry, inp):
    C, n, m = carry
    q, k, v, ig, lf = inp
    L = q.shape[2]
    b = jnp.cumsum(lf, axis=-1)
    causal = jnp.tril(jnp.ones((L, L), dtype=bool))
    dmat = jnp.where(causal, b[..., :, None] - b[..., None, :] + ig[..., None, :], -jnp.inf)
    inter = b + m[..., None]
    m_hat = jnp.maximum(inter, jnp.max(dmat, axis=-1))
    s = jnp.einsum('bhtd,bhsd->bhts', q, k) * jnp.exp(dmat - m_hat[..., None])
    w_inter = jnp.exp(inter - m_hat)
    num = jnp.einsum('bhts,bhse->bhte', s, v) + w_inter[..., None] * jnp.einsum('bhtd,bhde->bhte', q, C)
    dot = jnp.sum(s, axis=-1) + w_inter * jnp.einsum('bhtd,bhd->bht', q, n)
    h = num / jnp.maximum(jnp.abs(dot), jnp.exp(-m_hat))[..., None]
    b_last = b[..., -1]
    d_end = b_last[..., None] - b + ig
    m_new = jnp.maximum(b_last + m, jnp.max(d_end, axis=-1))
    w_end = jnp.exp(d_end - m_new[..., None])
    decay = jnp.exp(b_last + m - m_new)
    C_new = decay[..., None, None] * C + jnp.einsum('bhs,bhsd,bhse->bhde', w_end, k, v)
    n_new = decay[..., None] * n + jnp.einsum('bhs,bhsd->bhd', w_end, k)
    return (C_new, n_new, m_new), h


def mlstm_branch(mq, mk, mv, mi, mf, mo, mz, C0, n0, m0, b_i, b_f, g_mn):
    f32 = jnp.float32
    B, T, _ = mq.shape

    def heads(t, d):
        return t.reshape(B, T, M_HEADS, d).transpose(0, 2, 1, 3).astype(f32)

    q = heads(mq, M_DK) * (M_DK ** -0.5)
    k = heads(mk, M_DK)
    v = heads(mv, M_DV)
    ig = (mi.astype(f32) + b_i.astype(f32)).transpose(0, 2, 1)
    lf = jax.nn.log_sigmoid(mf.astype(f32) + b_f.astype(f32)).transpose(0, 2, 1)
    L = M_CHUNK if T % M_CHUNK == 0 else T
    nc = T // L

    def chunks(t):
        return jnp.moveaxis(t.reshape(t.shape[:2] + (nc, L) + t.shape[3:]), 2, 0)

    (C, n, m), h = lax.scan(mlstm_chunk, (C0.astype(f32), n0.astype(f32), m0.astype(f32)),
                            (chunks(q), chunks(k), chunks(v), chunks(ig), chunks(lf)))
    h = jnp.moveaxis(h, 0, 2).reshape(B, M_HEADS, T, M_DV).transpose(0, 2, 1, 3)
    h = h * lax.rsqrt(jnp.mean(h * h, axis=-1, keepdims=True) + NORM_EPS)
    h = h.reshape(B, T, M_WIDTH) * g_mn.astype(f32)
    h = h * jax.nn.sigmoid(mo.astype(f32)) * jax.nn.silu(mz.astype(f32))
    return h.astype(mq.dtype), C, n, m


def sink_attend(q, k, v, q_pos, k_pos, sinks):
    f32 = jnp.float32
    B, N, Tq = q.shape[:3]
    qg = q.reshape(B, N, Tq, A_KV_HEADS, A_GROUP, A_HEAD_DIM).astype(f32)
    s = jnp.einsum('bnqkgd,bnskd->bnkgqs', qg, k.astype(f32)) * (A_HEAD_DIM ** -0.5)
    dist = q_pos[:, :, None] - k_pos[:, None, :]
    mask = (dist >= 0) & (dist < WINDOW) & (k_pos[:, None, :] >= 0)
    s = jnp.where(mask[None, :, None, None], s, -jnp.inf)
    sink = sinks.astype(f32).reshape(A_KV_HEADS, A_GROUP)[None, None, :, :, None, None]
    mx = jnp.maximum(jnp.max(s, axis=-1, keepdims=True), sink)
    p = jnp.exp(s - mx)
    den = jnp.sum(p, axis=-1, keepdims=True) + jnp.exp(sink - mx)
    o = jnp.einsum('bnkgqs,bnskd->bnqkgd', p / den, v.astype(f32))
    return o.reshape(B, N, Tq, A_WIDTH)


def attn_prompt(q, k, v, sinks):
    B, T = q.shape[:2]
    nb = T // WINDOW
    qb = q.reshape(B, nb, WINDOW, A_HEADS, A_HEAD_DIM)

    def with_prev(t):
        tb = t.reshape(B, nb, WINDOW, A_KV_HEADS, A_HEAD_DIM)
        prev = jnp.pad(tb, ((0, 0), (1, 0), (0, 0), (0, 0), (0, 0)))[:, :-1]
        return jnp.concatenate([prev, tb], axis=2)

    pos = jnp.arange(T, dtype=jnp.int32).reshape(nb, WINDOW)
    k_pos = jnp.concatenate([pos - WINDOW, pos], axis=1)
    o = sink_attend(qb, with_prev(k), with_prev(v), pos, k_pos, sinks)
    return o.reshape(B, T, A_WIDTH)


def layer(x, c, C0, n0, m0, k_past, v_past, w_ada, b_ada, g_norm, w_in, b_i, b_f, g_mn,
          sinks, w_mo, w_ao, w_o):
    B, T, _ = x.shape
    h, gate = adaln_in(x, c, w_ada, b_ada, g_norm)
    (mq, mk, mv, mi, mf, mo, mz, aq, ak, av, az, gm, ga) = split_in(h @ w_in)
    y_m, C, n, m = mlstm_branch(mq, mk, mv, mi, mf, mo, mz, C0, n0, m0, b_i, b_f, g_mn)
    if k_past is None:
        pos = jnp.arange(T, dtype=jnp.int32)
    else:
        pos = PAST_LEN + jnp.arange(T, dtype=jnp.int32)
    q = partial_rope(aq.reshape(B, T, A_HEADS, A_HEAD_DIM), pos)
    k = partial_rope(ak.reshape(B, T, A_KV_HEADS, A_HEAD_DIM), pos)
    v = av.reshape(B, T, A_KV_HEADS, A_HEAD_DIM)
    if k_past is None:
        o = attn_prompt(q, k, v, sinks)
        k_keep, v_keep = k[:, -WINDOW:], v[:, -WINDOW:]
    else:
        cw = k_past.shape[1]
        kk = jnp.concatenate([k_past.astype(k.dtype), k], axis=1)
        vv = jnp.concatenate([v_past.astype(v.dtype), v], axis=1)
        k_pos = jnp.concatenate([PAST_LEN - cw + jnp.arange(cw, dtype=jnp.int32), pos])
        o = sink_attend(q[:, None], kk[:, None], vv[:, None], pos[None], k_pos[None], sinks)[:, 0]
        k_keep, v_keep = kk[:, -cw:], vv[:, -cw:]
    y_a = o.astype(x.dtype) * jax.nn.silu(az)
    u = jax.nn.sigmoid(gm) * (y_m @ w_mo) + jax.nn.sigmoid(ga) * (y_a @ w_ao)
    x = x + gate * (u @ w_o)
    return x, C, n, m, k_keep, v_keep


def setup_inputs(seed: int = 0) -> dict:
    key = jax.random.key(seed)
    ks = jax.random.split(key, 24)
    f32 = jnp.float32
    cw = min(WINDOW, PAST_LEN)
    nrm = lambda k, shape, s: jax.random.normal(k, shape, f32) * s
    b_f = jnp.linspace(3.0, 6.0, M_HEADS, dtype=f32)[None, :] + nrm(ks[17], (DEPTH, M_HEADS), 0.1)
    return {
        'x_prompt': nrm(ks[0], (BATCH, SEQ, D_MODEL), 1.0),
        'x_sample': nrm(ks[1], (DEC_BATCH, DEC_SEQ, D_MODEL), 1.0),
        'state_C': nrm(ks[2], (DEPTH, DEC_BATCH, M_HEADS, M_DK, M_DV), 1.0),
        'state_n': nrm(ks[3], (DEPTH, DEC_BATCH, M_HEADS, M_DK), 1.0),
        'state_m': jax.random.uniform(ks[4], (DEPTH, DEC_BATCH, M_HEADS), f32, 0.0, 2.0),
        'cache_k': nrm(ks[5], (DEPTH, DEC_BATCH, cw, A_KV_HEADS, A_HEAD_DIM), 1.0),
        'cache_v': nrm(ks[6], (DEPTH, DEC_BATCH, cw, A_KV_HEADS, A_HEAD_DIM), 1.0),
        'c_prompt': nrm(ks[7], (BATCH, D_MODEL), 1.0),
        'c_sample': nrm(ks[8], (DEC_BATCH, D_MODEL), 1.0),
        'w_ada': nrm(ks[9], (DEPTH, D_MODEL, 3 * D_MODEL), 0.5 * D_MODEL ** -0.5),
        'b_ada': nrm(ks[10], (DEPTH, 3 * D_MODEL), 0.02),
        'g_norm': 1.0 + nrm(ks[11], (DEPTH, D_MODEL), 0.02),
        'w_in': nrm(ks[12], (DEPTH, D_MODEL, IN_COLS), D_MODEL ** -0.5),
        'b_igate': nrm(ks[13], (DEPTH, M_HEADS), 0.1) - 1.0,
        'b_fgate': b_f,
        'g_mnorm': 1.0 + nrm(ks[14], (DEPTH, M_WIDTH), 0.02),
        'sinks': nrm(ks[15], (DEPTH, A_HEADS), 1.0),
        'w_m_out': nrm(ks[16], (DEPTH, M_WIDTH, D_MODEL), M_WIDTH ** -0.5),
        'w_a_out': nrm(ks[18], (DEPTH, A_WIDTH, D_MODEL), A_WIDTH ** -0.5),
        'w_out': nrm(ks[19], (DEPTH, D_MODEL, D_MODEL), D_MODEL ** -0.5),
        'g_final': 1.0 + nrm(ks[20], (D_MODEL,), 0.02),
    }


def reference(x_prompt, x_sample, state_C, state_n, state_m, cache_k, cache_v, c_prompt, c_sample,
              w_ada, b_ada, g_norm, w_in, b_igate, b_fgate, g_mnorm, sinks, w_m_out, w_a_out,
              w_out, g_final):
    f32 = jnp.float32
    bp = x_prompt.shape[0]
    yp, ys = x_prompt, x_sample
    sp, ss = [], []
    for l in range(DEPTH):
        zC = jnp.zeros((bp, M_HEADS, M_DK, M_DV), f32)
        zn = jnp.zeros((bp, M_HEADS, M_DK), f32)
        zm = jnp.zeros((bp, M_HEADS), f32)
        yp, Cp, np_, mp, kp, vp = layer(yp, c_prompt, zC, zn, zm, None, None,
                                        w_ada[l], b_ada[l], g_norm[l], w_in[l], b_igate[l], b_fgate[l],
                                        g_mnorm[l], sinks[l], w_m_out[l], w_a_out[l], w_out[l])
        ys, Cs, ns, ms, ksm, vsm = layer(ys, c_sample, state_C[l], state_n[l], state_m[l],
                                         cache_k[l], cache_v[l],
                                         w_ada[l], b_ada[l], g_norm[l], w_in[l], b_igate[l], b_fgate[l],
                                         g_mnorm[l], sinks[l], w_m_out[l], w_a_out[l], w_out[l])
        sp.append((Cp, np_, mp, kp, vp))
        ss.append((Cs, ns, ms, ksm, vsm))
    y_prompt = rmsnorm(yp, g_final)
    y_sample = rmsnorm(ys, g_final)
    C_p = jnp.stack([s[0] for s in sp])
    n_p = jnp.stack([s[1] for s in sp])
    m_p = jnp.stack([s[2] for s in sp])
    k_p = jnp.stack([s[3] for s in sp])
    v_p = jnp.stack([s[4] for s in sp])
    C_s = jnp.stack([s[0] for s in ss])
    n_s = jnp.stack([s[1] for s in ss])
    m_s = jnp.stack([s[2] for s in ss])
    k_s = jnp.stack([s[3] for s in ss])
    v_s = jnp.stack([s[4] for s in ss])
    return (y_prompt, y_sample, C_p, n_p, m_p, k_p, v_p, C_s, n_s, m_s, k_s, v_s)
```

```python
import contextlib
import numpy as np
import concourse.bass as bass
import concourse.mybir as mybir
from concourse.bass_utils import run_bass_kernel_spmd

F32 = mybir.dt.float32
BF16 = mybir.dt.bfloat16
AF = mybir.ActivationFunctionType
ALU = mybir.AluOpType
AX = mybir.AxisListType

NCORES = 8
DEBUG = False
NT_RUN = 16
RUN_SAMPLE = True
SELF_SYNC_ENGS = ("act", "dve", "pool")
TRACE = False
SAMPLE_STAGE = 9
NRING = 9
PFD = 4
MIXSTEPS = 2
FSTEPS = 1
DBG_NAMES = []
D = 1024
SEQ = 2048
NT = SEQ // 128
NCOL = 5896
EPS = 1e-6
LN_EPS = -13.815510557964274
PAST = 16384
O_MQ, O_MK, O_MV, O_IG, O_MO, O_MZ, O_AQ, O_AK, O_AZ, O_GM, O_GA = 0, 512, 1024, 1536, 1544, 2056, 2568, 3080, 3336, 3848, 4872

C_ID, C_LE, C_GT, C_ONE, C_BC, C_BS, C_P1, C_P2, C_P3, C_BM, C_BM0, C_SEL, C_BEXP, C_RP, C_RS = (
    0, 128, 256, 384, 512, 576, 640, 704, 768, 832, 848, 864, 1056, 1120, 1376)
CW = 1392


class Res:
    __slots__ = ("name", "lw", "rd", "sem", "dcount")

    def __init__(self, name):
        self.name = name
        self.lw = None
        self.rd = []
        self.sem = None
        self.dcount = 0


class Prog:
    ENG = {"pe": "tensor", "act": "scalar", "dve": "vector", "pool": "gpsimd", "sp": "sync"}

    def __init__(self, nc, stack):
        self.nc = nc
        self.stack = stack
        self.cnt = {e: 0 for e in self.ENG}
        self.esem = {e: stack.enter_context(nc.semaphore("es_" + e)) for e in self.ENG if e != "sp"}
        self.waited = {e: {} for e in self.ENG}
        self.dres = []

    def op(self, eng, fn, reads=(), writes=(), dma=None, signal=True):
        deps = []
        for r in reads:
            if r.lw is not None:
                deps.append(r.lw)
        for w in writes:
            if w.lw is not None:
                deps.append(w.lw)
            deps.extend(w.rd)
        need = {}
        for tok in deps:
            if tok[0] == "e" and tok[1] == eng and (eng == "pe" or eng not in SELF_SYNC_ENGS):
                continue
            k = ("e", tok[1]) if tok[0] == "e" else ("d", id(tok[1]))
            if k not in need or need[k][2] < tok[2]:
                need[k] = tok
        e = getattr(self.nc, self.ENG[eng])
        for k, tok in need.items():
            if self.waited[eng].get(k, -1) >= tok[2]:
                continue
            self.waited[eng][k] = tok[2]
            e.wait_ge(self.esem[tok[1]] if tok[0] == "e" else tok[1].sem, tok[2])
        if dma is not None:
            if dma.sem is None:
                dma.sem = self.stack.enter_context(self.nc.semaphore("ds_" + dma.name))
                self.dres.append(dma)
            dma.dcount += 1
            tok = ("d", dma, dma.dcount * 16)
            fn(e).then_inc(dma.sem, 16)
        elif not signal:
            assert eng == "pe"
            tok = ("e", eng, self.cnt[eng] + 1)
            fn(e)
        else:
            self.cnt[eng] += 1
            tok = ("e", eng, self.cnt[eng])
            fn(e).then_inc(self.esem[eng], 1)
        for r in reads:
            r.rd.append(tok)
        for w in writes:
            w.lw = tok
            w.rd = []
        return tok

    def barrier(self, skip=()):
        for eng in self.ENG:
            e = getattr(self.nc, self.ENG[eng])
            for r in self.dres:
                if r in skip:
                    continue
                k = ("d", id(r))
                v = r.dcount * 16
                if v > 0 and self.waited[eng].get(k, -1) < v:
                    self.waited[eng][k] = v
                    e.wait_ge(r.sem, v)
            for en in ("pe", "act", "dve", "pool"):
                if en == eng:
                    continue
                k = ("e", en)
                v = self.cnt[en]
                if v > 0 and self.waited[eng].get(k, -1) < v:
                    self.waited[eng][k] = v
                    e.wait_ge(self.esem[en], v)


def make_consts():
    c = np.zeros((128, CW), np.float32)
    i = np.arange(128)
    c[:, C_ID:C_ID + 128] = np.eye(128, dtype=np.float32)
    c[:, C_LE:C_LE + 128] = (i[:, None] <= i[None, :])
    c[:, C_GT:C_GT + 128] = (i[:, None] > i[None, :])
    c[:, C_ONE:C_ONE + 128] = 1.0
    j = np.arange(64)
    same = (j[:, None] // 4 == j[None, :] // 4)
    c[:64, C_BC:C_BC + 64] = same & (j[:, None] <= j[None, :])
    c[:64, C_BS:C_BS + 64] = same
    for k, off in ((1, C_P1), (2, C_P2), (3, C_P3)):
        src = 4 * (j // 4) + (j + k) % 4
        c[src, off + j] = 1.0
    b = np.arange(16)
    c[:64, C_BM:C_BM + 16] = (j[:, None] // 4 == b[None, :])
    c[:64, C_BM0:C_BM0 + 16] = (j[:, None] == 4 * b[None, :])
    c[0, C_SEL:C_SEL + 128] = 1.0
    for bb in range(16):
        c[1 + bb, C_SEL + 128 + 4 * bb:C_SEL + 128 + 4 * bb + 4] = 1.0
        c[bb, C_BEXP + 4 * bb:C_BEXP + 4 * bb + 4] = 1.0
    inv = (500000.0 ** (-np.arange(0, 16, 2, dtype=np.float32) / 16)).astype(np.float32)
    pos = (np.arange(NT)[None, :] * 128 + i[:, None]).astype(np.float32)
    ang = pos[:, :, None] * inv[None, None, :]
    rp = np.concatenate([np.cos(ang), np.sin(ang)], axis=-1).astype(np.float32)
    c[:, C_RP:C_RP + NT * 16] = rp.reshape(128, NT * 16)
    poss = (PAST + (j % 4)).astype(np.float32)
    angs = poss[:, None] * inv[None, :]
    c[:64, C_RS:C_RS + 16] = np.concatenate([np.cos(angs), np.sin(angs)], axis=-1)
    return c


def build():
    nc = bass.Bass("TRN2", target_bir_lowering=False)

    def din(name, shape):
        return nc.dram_tensor(name, list(shape), F32, kind="ExternalInput").ap()

    def dout(name, shape):
        return nc.dram_tensor(name, list(shape), F32, kind="ExternalOutput").ap()

    xp_d = din("xp", [SEQ, D]); xs_d = din("xs", [64, D]); cc_d = din("cc", [17, D])
    stC_d = din("stC", [16, 4, 128, 128]); stn_d = din("stn", [64, 128]); stm_d = din("stm", [16, 4])
    ck_d = din("ck", [16, 128, 128]); cv_d = din("cv", [16, 128, 128])
    wada_d = din("w_ada", [D, 3 * D]); bada_d = din("b_ada", [1, 3 * D]); gn_d = din("g_norm", [1, D])
    win_d = din("w_in", [D, NCOL]); bi_d = din("b_i", [1, 4]); bf_d = din("b_f", [1, 4])
    gmn_d = din("g_mn", [1, 512]); sk_d = din("sinks", [1, 8])
    wmo_d = din("w_mo", [512, D]); wao_d = din("w_ao", [512, D]); wo_d = din("w_o", [D, D])
    gf_d = din("g_final", [1, D]); cst_d = din("cst", [128, CW])
    yp_d = dout("yp", [SEQ, D]); ys_d = dout("ys", [64, D])
    Cp_d = dout("Cp", [4, 128, 128]); np_d = dout("np", [4, 128]); mp_d = dout("mp", [1, 4])
    kp_d = dout("kp", [128, 128]); vp_d = dout("vp", [128, 128])
    Cs_d = dout("Cs", [16, 4, 128, 128]); ns_d = dout("ns", [64, 128]); ms_d = dout("ms", [16, 4])
    ks_d = dout("ks", [16, 128, 128]); vs_d = dout("vs", [16, 128, 128])

    with contextlib.ExitStack() as st:
        P = Prog(nc, st)
        ncnt = [0]
        dbg_list = []

        def dump(name, ap, res):
            if not DEBUG:
                return
            shp = [int(x) for x in ap.shape]
            dt_ = ap.dtype
            if dt_ != F32:
                return
            d = nc.dram_tensor("dbg_" + name, shp, dt_, kind="ExternalOutput").ap()
            rr = Res("dbg_" + name)
            P.op("sp", lambda e: e.dma_start(out=d, in_=ap), reads=[res], dma=rr)
            DBG_NAMES.append("dbg_" + name)

        def T(shape, dt, name=None):
            ncnt[0] += 1
            nm = (name or "t") + "_%d" % ncnt[0]
            return st.enter_context(nc.sbuf_tensor(nm, list(shape), dt)), Res(nm)

        def act_accum(out, in_, acc, reads, writes):
            P.op("act", lambda e: e.activation(out=out, in_=in_, func=AF.Square), reads=reads, writes=writes)
            P.op("dve", lambda e: e.tensor_reduce(out=acc, in_=out, axis=AX.X, op=ALU.add), reads=writes, writes=writes)

        Wb, rWb = T([128, 8, NCOL], BF16, "Wb")
        edges = [0, 512, 1024, 1536, 2056, 2568, 3080, 3336, 3848, 4360, 4872, 5384, 5896]
        rWbp = [Res("Wb%d" % j) for j in range(len(edges) - 1)]
        wmo, rwmo = T([128, 4, D], BF16, "wmo")
        wao, rwao = T([128, 4, D], BF16, "wao")
        wo, rwo = T([128, 8, D], BF16, "wo")
        cst, rcst = T([128, CW], F32, "cst")
        cbf, rcbf = T([128, 640], BF16, "cbf")
        gfin, rgfin = T([128, D], F32, "gfin")
        gmnq, rgmnq = T([128, 512], BF16, "gmnq")
        esk, resk = T([128, 8], F32, "esk")
        bif, rbif = T([128, 8], F32, "bif")
        nhalf, rnhalf = T([128, 8], F32, "nhalf")
        modT, rmodT = T([128, 24, 17], F32, "modT")
        GT, rGT = T([128, 8, 17], F32, "GT")
        gnT, rgnT = T([128, 8], F32, "gnT")
        badT, rbadT = T([128, 24], F32, "badT")
        gateh, rgateh = T([128, D], F32, "gateh")
        gatehs, rgatehs = T([64, D], BF16, "gatehs")
        pb = []
        for i in range(8):
            t_ = st.enter_context(nc.psum_tensor("pb%d" % i, [128, 512], F32))
            pb.append((t_, t_.bitcast(BF16), Res("pb%d" % i)))
        zb_i = [0]
        gb_i = [0]

        def zbank():
            zb_i[0] = (zb_i[0] + 1) % 2
            return pb[zb_i[0]]

        def gbank():
            gb_i[0] = (gb_i[0] + 1) % 5
            return pb[3 + gb_i[0]]

        def tbank():
            zb_i[0] = (zb_i[0] + 1) % 3
            return pb[zb_i[0]]

        ident = cbf[:, 0:128]
        identf = cst[:, C_ID:C_ID + 128]
        onesf = cst[:, C_ONE:C_ONE + 128]
        onesb = cbf[:, 384:512]

        P.op("sp", lambda e: e.dma_start(out=cst[:], in_=cst_d), writes=[rcst], dma=rcst)
        winv = win_d.rearrange("(k p) c -> p k c", p=128)
        P.op("sp", lambda e: e.dma_start(out=gfin[:], in_=gf_d.broadcast_to([128, D])), writes=[rgfin], dma=rgfin)
        P.op("pool", lambda e: e.dma_start(out=gmnq[:], in_=gmn_d.broadcast_to([128, 512])), writes=[rgmnq], dma=rgmnq)
        P.op("sp", lambda e: e.dma_start(out=esk[:], in_=sk_d.broadcast_to([128, 8])), writes=[resk], dma=resk)
        P.op("sp", lambda e: e.dma_start(out=bif[:, 0:4], in_=bi_d.broadcast_to([128, 4])), writes=[rbif], dma=rbif)
        P.op("sp", lambda e: e.dma_start(out=bif[:, 4:8], in_=bf_d.broadcast_to([128, 4])), writes=[rbif], dma=rbif)
        P.op("pool", lambda e: e.memset(nhalf[:], -0.5), writes=[rnhalf])
        P.op("dve", lambda e: e.tensor_copy(out=cbf[:], in_=cst[:, 0:640]), reads=[rcst], writes=[rcbf])
        P.op("dve", lambda e: e.tensor_scalar(out=gmnq[:], in0=gmnq[:], scalar1=0.25, scalar2=None, op0=ALU.mult),
             reads=[rgmnq], writes=[rgmnq])
        P.op("act", lambda e: e.activation(out=esk[:], in_=esk[:], func=AF.Exp), reads=[resk], writes=[resk])

        with contextlib.ExitStack() as s0:
            def T0(shape, dt, name):
                ncnt[0] += 1
                nm = name + "_%d" % ncnt[0]
                return s0.enter_context(nc.sbuf_tensor(nm, list(shape), dt)), Res(nm)
            c17, rc17 = T0([17, D], F32, "c17")
            c17t, rc17t = T0([17, D], F32, "c17t")
            c17b, rc17b = T0([17, D], BF16, "c17b")
            cT, rcT = T0([128, 8, 17], BF16, "cT")
            bgb, rbgb = T0([17, D], F32, "bgb")
            modg, rmodg = T0([17, D], F32, "modg")
            ld24, rld24 = T0([32, 128], F32, "ld24")
            was = [T0([128, 8, 512], BF16, "was%d" % i) for i in range(4)]
            P.op("sp", lambda e: e.dma_start(out=c17[:], in_=cc_d), writes=[rc17], dma=rc17)
            wadav = wada_d.rearrange("(k p) c -> p k c", p=128)
            for j in range(4):
                P.op("pool", lambda e, j=j: e.dma_start(out=was[j][0][:], in_=wadav[:, :, j * 512:(j + 1) * 512]), writes=[was[j][1]], dma=was[j][1])
            P.op("act", lambda e: e.activation(out=c17t[:], in_=c17[:], func=AF.Tanh, scale=0.5), reads=[rc17], writes=[rc17t])
            P.op("dve", lambda e: e.scalar_tensor_tensor(out=c17t[:], in0=c17t[:], scalar=1.0, in1=c17[:], op0=ALU.add, op1=ALU.mult),
                 reads=[rc17t, rc17], writes=[rc17t])
            P.op("dve", lambda e: e.tensor_scalar(out=c17b[:], in0=c17t[:], scalar1=0.5, scalar2=None, op0=ALU.mult),
                 reads=[rc17t], writes=[rc17b])
            bt, bb, rb = gbank()
            for k in range(8):
                P.op("pe", lambda e, k=k: e.transpose(out=bb[:, k * 32:k * 32 + 17], in_=c17b[:, k * 128:(k + 1) * 128], identity=ident[0:17, 0:17]),
                     reads=[rc17b, rcbf], writes=[rb])
            P.op("dve", lambda e: e.tensor_copy(out=cT[:], in_=bb[:, 0:256].rearrange("p (k r) -> p k r", r=32)[:, :, 0:17]),
                 reads=[rb], writes=[rb, rcT])
            P.op("sp", lambda e: e.dma_start(out=ld24[0:24, :], in_=bada_d.rearrange("o (j p) -> (o j) p", p=128)), writes=[rld24], dma=rld24)
            P.op("sp", lambda e: e.dma_start(out=ld24[24:32, :], in_=gn_d.rearrange("o (j p) -> (o j) p", p=128)), writes=[rld24], dma=rld24)
            bt, bb, rb = gbank()
            P.op("pe", lambda e: e.transpose(out=bt[:, 0:32], in_=ld24[:], identity=identf[0:32, 0:32]), reads=[rld24, rcst], writes=[rb])
            P.op("dve", lambda e: e.tensor_copy(out=badT[:], in_=bt[:, 0:24]), reads=[rb], writes=[rb, rbadT])
            P.op("dve", lambda e: e.tensor_copy(out=gnT[:], in_=bt[:, 24:32]), reads=[rb], writes=[rb, rgnT])
            P.op("sp", lambda e: e.dma_start(out=bgb[:], in_=bada_d[:, 2 * D:3 * D].broadcast_to([17, D])), writes=[rbgb], dma=rbgb)
            for j in range(6):
                wa, rwa = was[j % 4]
                if j >= 4:
                    P.op("pool", lambda e, wa=wa, j=j: e.dma_start(out=wa[:], in_=wadav[:, :, j * 512:(j + 1) * 512]), writes=[rwa], dma=rwa)
                if j < 4:
                    bt, bb, rb = gbank()
                    for sub in range(4):
                        for k in range(8):
                            P.op("pe", lambda e, wa=wa, sub=sub, k=k, bt=bt: e.matmul(
                                bt[:, sub * 32:sub * 32 + 17], lhsT=wa[:, k, sub * 128:(sub + 1) * 128], rhs=cT[:, k, :],
                                start=(k == 0), stop=(k == 7)), reads=[rwa, rcT], writes=[rb])
                    for sub in range(4):
                        jj = j * 4 + sub
                        P.op("dve", lambda e, bt=bt, sub=sub, jj=jj: e.tensor_scalar(
                            out=modT[:, jj, :], in0=bt[:, sub * 32:sub * 32 + 17], scalar1=badT[:, jj:jj + 1], scalar2=None, op0=ALU.add),
                            reads=[rb, rbadT], writes=[rb, rmodT])
                else:
                    bt, bb, rb = gbank()
                    for k in range(8):
                        P.op("pe", lambda e, wa=wa, k=k, bt=bt: e.matmul(bt[0:17, :], lhsT=cT[:, k, :], rhs=wa[:, k, :],
                                                                       start=(k == 0), stop=(k == 7)), reads=[rwa, rcT], writes=[rb])
                    P.op("dve", lambda e, bt=bt, j=j: e.tensor_tensor(out=modg[:, (j - 4) * 512:(j - 3) * 512], in0=bt[0:17, :],
                                                                     in1=bgb[:, (j - 4) * 512:(j - 3) * 512], op=ALU.add),
                         reads=[rb, rbgb], writes=[rb, rmodg])
            P.op("dve", lambda e: e.scalar_tensor_tensor(out=GT[:], in0=modT[:, 8:16, :], scalar=1.0,
                                                         in1=gnT[:, :, None].broadcast_to([128, 8, 17]), op0=ALU.add, op1=ALU.mult),
                 reads=[rmodT, rgnT], writes=[rGT])
            for half in range(2):
                bt, bb, rb = gbank()
                P.op("pe", lambda e, bt=bt, half=half: e.matmul(bt[:, :], lhsT=cst[0:17, C_SEL:C_SEL + 128], rhs=modg[:, half * 512:(half + 1) * 512],
                                                               start=True, stop=True), reads=[rcst, rmodg], writes=[rb])
                P.op("act", lambda e, bt=bt, half=half: e.activation(out=gateh[:, half * 512:(half + 1) * 512], in_=bt[:, :], func=AF.Copy, scale=0.5),
                     reads=[rb], writes=[rb, rgateh])
                bt, bb, rb = gbank()
                P.op("pe", lambda e, bt=bt, half=half: e.matmul(bt[0:64, :], lhsT=cst[0:17, C_SEL + 128:C_SEL + 192], rhs=modg[:, half * 512:(half + 1) * 512],
                                                               start=True, stop=True), reads=[rcst, rmodg], writes=[rb])
                P.op("act", lambda e, bt=bt, half=half: e.activation(out=gatehs[:, half * 512:(half + 1) * 512], in_=bt[0:64, :], func=AF.Copy, scale=0.5),
                     reads=[rb], writes=[rb, rgatehs])
            for j_, (a_, b_) in enumerate(zip(edges[:-1], edges[1:])):
                P.op("pool", lambda e, a_=a_, b_=b_: e.dma_start(out=Wb[:, :, a_:b_], in_=winv[:, :, a_:b_]), writes=[rWbp[j_]], dma=rWbp[j_])
            P.op("pool", lambda e: e.dma_start(out=wmo[:], in_=wmo_d.rearrange("(k p) c -> p k c", p=128)), writes=[rwmo], dma=rwmo)
            P.op("pool", lambda e: e.dma_start(out=wao[:], in_=wao_d.rearrange("(k p) c -> p k c", p=128)), writes=[rwao], dma=rwao)
            P.op("pool", lambda e: e.dma_start(out=wo[:], in_=wo_d.rearrange("(k p) c -> p k c", p=128)), writes=[rwo], dma=rwo)
            dump("modT", modT[:], rmodT); dump("GT", GT[:], rGT); dump("gateh", gateh[:], rgateh); dump("cT", cT[:], rcT)
            dump("modg", modg[:], rmodg); dump("badT", badT[:], rbadT); dump("gnT", gnT[:], rgnT)
            P.barrier(skip=rWbp + [rwmo, rwao, rwo])

        xs2 = [T([128, D], F32, "x%d" % i) for i in range(2)]
        xnb, rxnb = T([128, D], BF16, "xnb")
        htmp, rhtmp = T([128, 512], F32, "htmp")
        htmpF, rhtmpF = T([128, 512], F32, "htmpF")
        smF, rsmF = T([128, 8], F32, "smF")
        rtmpF, rrtmpF = htmpF[:, 0:320].rearrange("p (a h d) -> p a h d", a=4, h=10), rhtmpF
        tht, rth = T([128, 512], BF16, "tht")
        thg, rthg = T([128, 4, 512], BF16, "thg")
        t1z, rt1z = T([128, 512], BF16, "t1z")
        A, rA = T([128, 768], F32, "A")
        vaug = [T([128, 2, 65], BF16, "vaug%d" % i) for i in range(3)]
        fsets = []

        def alloc_fset():
            d = {}
            d["hT"], d["rhT"] = T([128, 8, 128], BF16, "hT")
            d["qkv"], d["rqkv"] = T([128, 3, 512], BF16, "qkv")
            d["G8"], d["rG8"] = T([128, 8], F32, "G8")
            d["Gm"], d["rGm"] = T([128, 512], BF16, "Gm")
            d["Ga"], d["rGa"] = T([128, 512], BF16, "Ga")
            d["qkb"], d["rqkb"] = T([128, 640], BF16, "qkb")
            fsets.append(d)
        alloc_fset()
        hT = rhT = qkv = rqkv = G8 = rG8 = Gm = rGm = Ga = rGa = qkb = rqkb = None

        def use_set(p):
            nonlocal hT, rhT, qkv, rqkv, G8, rG8, Gm, rGm, Ga, rGa, qkb, rqkb
            d = fsets[p]
            hT, rhT, qkv, rqkv, G8, rG8 = d["hT"], d["rhT"], d["qkv"], d["rqkv"], d["G8"], d["rG8"]
            Gm, rGm, Ga, rGa, qkb, rqkb = d["Gm"], d["rGm"], d["Ga"], d["rGa"], d["qkb"], d["rqkb"]
        use_set(0)
        akT = [T([128, 128], BF16, "akT%d" % i) for i in range(2)]
        aqT, raqT = T([128, 4, 128], BF16, "aqT")
        TR, rTR = T([128, 8, 128], BF16, "TR")
        kw, rkw = T([128, 512], BF16, "kw")
        Csb, rCsb = T([128, 4, 128], BF16, "Csb")
        nsb, rnsb = T([128, 4], BF16, "nsb")
        wm, rwm = T([128, 4, 128], BF16, "wm")
        Sp, rSp = T([128, 4, 128], BF16, "Sp")
        Cst, rCst = T([128, 4, 128], F32, "Cst")
        nst, rnst = T([128, 4], F32, "nst")
        ym, rym = T([128, 1024], BF16, "ym")
        Pc, rPc = T([128, 512], BF16, "Pc")
        Pp, rPp = T([128, 512], BF16, "Pp")
        t12, rt12 = T([128, 512], BF16, "t12")
        rtmp, rrtmp = rtmpF, rrtmpF
        u, ru = ym, rym
        t3, rt3 = htmp, rhtmp
        sm, rsm = T([128, 96], F32, "sm")
        smA, rsmA = T([128, 64], F32, "smA")
        smB, rsmB = T([128, 64], F32, "smB")
        mprev = [T([128, 4], F32, "mprev%d" % i) for i in range(2)]
        E12, rE12 = T([128, 16], F32, "E12")
        X12, rX12 = T([128, 16], F32, "X12")
        for (v_, rv_) in vaug:
            P.op("pool", lambda e, v_=v_: e.memset(v_[:], 1.0), writes=[rv_])

        def zmm(R, c0, c1):
            bt, bb, rb = zbank()
            rw = [rWbp[j] for j in range(len(edges) - 1) if edges[j] < c1 and edges[j + 1] > c0]
            for k in range(8):
                P.op("pe", lambda e, k=k, bt=bt: e.matmul(bt[0:R, 0:c1 - c0], lhsT=hT[:, k, 0:R], rhs=Wb[:, k, c0:c1],
                                                         start=(k == 0), stop=(k == 7)), reads=[rhT] + rw, writes=[rb], signal=(k == 7))
            return bt, rb

        def zgates(R):
            for j in range(4):
                bt, rb = zmm(R, O_GM + j * 512, O_GM + (j + 1) * 512)
                P.op("act", lambda e, bt=bt, j=j: e.activation(out=thg[0:R, j, :], in_=bt[0:R, :], func=AF.Tanh, scale=0.5), reads=[rb], writes=[rb, rthg])
                yield

        def front(R, x_ap_dram, xt, rxt, sample, ropeap, va_t=None, rva_t=None, last=False):
            P.op("act", lambda e: e.dma_start(out=xt[0:R, :], in_=x_ap_dram), writes=[rxt], dma=rxt)
            act_accum(htmpF[0:R, :], xt[0:R, 0:512], smF[0:R, 0:1], [rxt], [rhtmpF, rsmF])
            act_accum(htmpF[0:R, :], xt[0:R, 512:1024], smF[0:R, 1:2], [rxt], [rhtmpF, rsmF])
            P.op("dve", lambda e: e.tensor_tensor(out=smF[0:R, 2:3], in0=smF[0:R, 0:1], in1=smF[0:R, 1:2], op=ALU.add), reads=[rsmF], writes=[rsmF])
            P.op("dve", lambda e: e.tensor_scalar(out=smF[0:R, 2:3], in0=smF[0:R, 2:3], scalar1=1.0 / D, scalar2=EPS, op0=ALU.mult, op1=ALU.add),
                 reads=[rsmF], writes=[rsmF])
            P.op("pool", lambda e: e.tensor_tensor(out=smF[0:R, 3:4], in0=smF[0:R, 2:3], in1=nhalf[0:R, 0:1], op=ALU.pow), reads=[rsmF, rnhalf], writes=[rsmF])
            P.op("act", lambda e: e.activation(out=xnb[0:R, :], in_=xt[0:R, :], func=AF.Copy, scale=smF[0:R, 3:4]), reads=[rxt, rsmF], writes=[rxnb])
            bt, bb, rb = pb[2]
            for k in range(8):
                P.op("pe", lambda e, k=k: e.transpose(out=bb[:, k * 128:k * 128 + R], in_=xnb[0:R, k * 128:(k + 1) * 128], identity=ident[0:R, 0:R]),
                     reads=[rxnb, rcbf], writes=[rb], signal=(k == 7))
            if not sample:
                for k in range(8):
                    P.op("act", lambda e, k=k: e.activation(out=hT[:, k, :], in_=bb[:, k * 128:(k + 1) * 128], func=AF.Identity,
                                                            scale=GT[:, k, 0:1], bias=modT[:, k, 0:1]), reads=[rb, rGT, rmodT], writes=[rb, rhT])
            else:
                src = bb[:, :].rearrange("p (k t) -> p k t", t=128)[:, :, 0:64].rearrange("p k (b t) -> p k b t", t=4)
                for k in range(8):
                    P.op("dve", lambda e, k=k: e.tensor_tensor(out=htmpF[:, 0:64].rearrange("p (b t) -> p b t", t=4), in0=src[:, k],
                                                               in1=GT[:, k, 1:17, None].broadcast_to([128, 16, 4]), op=ALU.mult),
                         reads=[rb, rGT], writes=[rb, rhtmpF])
                    P.op("dve", lambda e, k=k: e.tensor_tensor(out=hT[:, k, 0:64].rearrange("p (b t) -> p b t", t=4),
                                                               in0=htmpF[:, 0:64].rearrange("p (b t) -> p b t", t=4),
                                                               in1=modT[:, k, 1:17, None].broadcast_to([128, 16, 4]), op=ALU.add),
                         reads=[rhtmpF, rmodT], writes=[rhtmpF, rhT])

            yield
            bt, rb = zmm(R, O_IG, O_IG + 8)
            P.op("dve", lambda e, bt=bt: e.tensor_copy(out=G8[0:R, :], in_=bt[0:R, 0:8]), reads=[rb], writes=[rb, rG8])
            yield
            bt, rb = zmm(R, O_MQ, O_MQ + 512)
            P.op("act", lambda e, bt=bt: e.activation(out=qkv[0:R, 0, :], in_=bt[0:R, :], func=AF.Copy, scale=128.0 ** -0.5), reads=[rb], writes=[rb, rqkv])
            yield
            bt, rb = zmm(R, O_MK, O_MK + 512)
            P.op("dve", lambda e, bt=bt: e.tensor_copy(out=qkv[0:R, 1, :], in_=bt[0:R, :]), reads=[rb], writes=[rb, rqkv])
            yield
            bt, rb = zmm(R, O_MV, O_MV + 512)
            P.op("act", lambda e, bt=bt: e.activation(out=qkv[0:R, 2, :], in_=bt[0:R, :], func=AF.Copy), reads=[rb], writes=[rb, rqkv])
            yield
            bt, rb = zmm(R, O_AQ, O_AQ + 512)
            P.op("dve", lambda e, bt=bt: e.tensor_copy(out=A[0:R, 0:512], in_=bt[0:R, :]), reads=[rb], writes=[rb, rA])
            yield
            bt, rb = zmm(R, O_AK, O_AK + 256)
            P.op("act", lambda e, bt=bt: e.activation(out=A[0:R, 512:768], in_=bt[0:R, 0:256], func=AF.Copy), reads=[rb], writes=[rb, rA])
            yield
            bt, rb = zmm(R, O_MZ, O_MZ + 512)
            P.op("act", lambda e, bt=bt: e.activation(out=tht[0:R, :], in_=bt[0:R, :], func=AF.Tanh, scale=0.5), reads=[rb], writes=[rb, rth])
            P.op("dve", lambda e, bt=bt: e.scalar_tensor_tensor(out=t1z[0:R, :], in0=tht[0:R, :], scalar=1.0, in1=bt[0:R, :], op0=ALU.add, op1=ALU.mult),
                 reads=[rb, rth], writes=[rb, rt1z])
            yield
            bt, rb = zmm(R, O_MO, O_MO + 512)
            P.op("act", lambda e, bt=bt: e.activation(out=tht[0:R, :], in_=bt[0:R, :], func=AF.Tanh, scale=0.5), reads=[rb], writes=[rb, rth])
            P.op("dve", lambda e: e.scalar_tensor_tensor(out=Gm[0:R, :], in0=tht[0:R, :], scalar=1.0, in1=t1z[0:R, :], op0=ALU.add, op1=ALU.mult),
                 reads=[rth, rt1z], writes=[rGm])
            P.op("pool", lambda e: e.tensor_tensor(out=Gm[0:R, :], in0=Gm[0:R, :], in1=gmnq[0:R, :], op=ALU.mult), reads=[rGm, rgmnq], writes=[rGm])
            yield
            bt, rb = zmm(R, O_AZ, O_AZ + 512)
            P.op("act", lambda e, bt=bt: e.activation(out=tht[0:R, :], in_=bt[0:R, :], func=AF.Tanh, scale=0.5), reads=[rb], writes=[rb, rth])
            P.op("dve", lambda e, bt=bt: e.scalar_tensor_tensor(out=Ga[0:R, :], in0=tht[0:R, :], scalar=1.0, in1=bt[0:R, :], op0=ALU.add, op1=ALU.mult),
                 reads=[rb, rth], writes=[rb, rGa])
            yield
            Av = A[0:R, 0:640].rearrange("p (h d) -> p h d", d=64)
            cosb = ropeap[:, None, 0:8].broadcast_to([R, 10, 8])
            sinb = ropeap[:, None, 8:16].broadcast_to([R, 10, 8])
            P.op("pool", lambda e: e.tensor_tensor(out=rtmp[0:R, 0], in0=Av[:, :, 0:8], in1=cosb, op=ALU.mult), reads=[rA, rcst], writes=[rrtmp])
            P.op("pool", lambda e: e.tensor_tensor(out=rtmp[0:R, 1], in0=Av[:, :, 8:16], in1=sinb, op=ALU.mult), reads=[rA, rcst], writes=[rrtmp])
            P.op("pool", lambda e: e.tensor_tensor(out=rtmp[0:R, 2], in0=Av[:, :, 8:16], in1=cosb, op=ALU.mult), reads=[rA, rcst], writes=[rrtmp])
            P.op("pool", lambda e: e.tensor_tensor(out=rtmp[0:R, 3], in0=Av[:, :, 0:8], in1=sinb, op=ALU.mult), reads=[rA, rcst], writes=[rrtmp])
            P.op("pool", lambda e: e.tensor_tensor(out=Av[:, :, 0:8], in0=rtmp[0:R, 0], in1=rtmp[0:R, 1], op=ALU.subtract), reads=[rrtmp], writes=[rA])
            P.op("pool", lambda e: e.tensor_tensor(out=Av[:, :, 8:16], in0=rtmp[0:R, 2], in1=rtmp[0:R, 3], op=ALU.add), reads=[rrtmp], writes=[rA])
            P.op("pool", lambda e: e.tensor_copy(out=qkb[0:R, 0:512].rearrange("p (j s d) -> p s j d", s=2, d=64),
                                                 in_=A[0:R, 0:512].rearrange("p (s j d) -> p s j d", s=2, d=64)), reads=[rA], writes=[rqkb])
            P.op("pool", lambda e: e.tensor_copy(out=qkb[0:R, 512:640], in_=A[0:R, 512:640]), reads=[rA], writes=[rqkb])
            if va_t is not None:
                P.op("pool", lambda e: e.tensor_copy(out=va_t[0:R, :, 0:64], in_=A[0:R, 640:768].rearrange("p (k d) -> p k d", d=64)), reads=[rA], writes=[rva_t])
            if last:
                P.op("sp", lambda e: e.dma_start(out=kp_d, in_=A[:, 512:640]), reads=[rA], dma=rA)
                P.op("sp", lambda e: e.dma_start(out=vp_d, in_=A[:, 640:768]), reads=[rA], dma=rA)
            yield

        def attn_transposes(R, akT_t, rakT, bank=None):
            bt, bb, rb = bank if bank is not None else gbank()
            for j in range(4):
                P.op("pe", lambda e, j=j: e.transpose(out=bb[:, j * 128:j * 128 + R], in_=qkb[0:R, j * 128:(j + 1) * 128], identity=ident[0:R, 0:R]),
                     reads=[rqkb, rcbf], writes=[rb], signal=False)
            P.op("pe", lambda e: e.transpose(out=bb[:, 512:512 + R], in_=qkb[0:R, 512:640], identity=ident[0:R, 0:R]), reads=[rqkb, rcbf], writes=[rb])
            P.op("act", lambda e: e.activation(out=aqT[:, :, 0:R], in_=bb[:, 0:512].rearrange("p (j t) -> p j t", t=128)[:, :, 0:R], func=AF.Copy),
                 reads=[rb], writes=[rb, raqT])
            P.op("act", lambda e: e.activation(out=akT_t[:, 0:R], in_=bb[:, 512:512 + R], func=AF.Copy), reads=[rb], writes=[rb, rakT])

        def mlstm_transposes(R, bank=None):
            bt, bb, rb = bank if bank is not None else gbank()
            for j in range(8):
                P.op("pe", lambda e, j=j: e.transpose(out=bb[:, j * 128:j * 128 + R], in_=qkv[0:R, j // 4, (j % 4) * 128:(j % 4 + 1) * 128],
                                                      identity=ident[0:R, 0:R]), reads=[rqkv, rcbf], writes=[rb], signal=(j == 7))
            P.op("act", lambda e: e.activation(out=TR[:, :, 0:R], in_=bb[:, :].rearrange("p (j t) -> p j t", t=128)[:, :, 0:R], func=AF.Copy),
                 reads=[rb], writes=[rb, rTR])

        def mlstm_finish(R, Nb_t, Db_ap, rbN, rbD, thr_ap):
            P.op("act", lambda e: e.activation(out=sm[0:R, 8:12], in_=Db_ap, func=AF.Square, scale=EPS ** 0.5), reads=[rbD], writes=[rbD, rsm])
            P.op("act", lambda e: e.activation(out=htmp[0:R, :], in_=Nb_t[0:R, 0:512], func=AF.Square), reads=[rbN], writes=[rbN, rhtmp])
            P.op("dve", lambda e: e.tensor_tensor(out=sm[0:R, 12:16], in0=sm[0:R, 8:12], in1=thr_ap, op=ALU.max), reads=[rsm, rX12], writes=[rsm])
            P.op("dve", lambda e: e.tensor_reduce(out=sm[0:R, 20:24], in_=htmp[0:R, :].rearrange("p (h d) -> p h d", d=128), axis=AX.X, op=ALU.add),
                 reads=[rhtmp], writes=[rhtmp, rsm])
            P.op("dve", lambda e: e.scalar_tensor_tensor(out=sm[0:R, 24:28], in0=sm[0:R, 20:24], scalar=1.0 / 128, in1=sm[0:R, 12:16], op0=ALU.mult, op1=ALU.add),
                 reads=[rsm], writes=[rsm])
            P.op("pool", lambda e: e.tensor_tensor(out=sm[0:R, 32:36], in0=sm[0:R, 24:28], in1=nhalf[0:R, 0:4], op=ALU.pow), reads=[rsm, rnhalf], writes=[rsm])
            for h in range(4):
                P.op("dve", lambda e, h=h: e.scalar_tensor_tensor(out=ym[0:R, h * 128:(h + 1) * 128], in0=Nb_t[0:R, h * 128:(h + 1) * 128],
                                                                  scalar=sm[0:R, 32 + h:33 + h], in1=Gm[0:R, h * 128:(h + 1) * 128], op0=ALU.mult, op1=ALU.mult),
                     reads=[rbN, rsm, rGm], writes=[rbN, rym])

        def attn_finish(R, banks):
            for kv in range(2):
                bt, rb = banks[kv]
                ov = bt[0:R, 0:260].rearrange("p (g d) -> p g d", d=65)
                P.op("dve", lambda e, ov=ov, kv=kv: e.tensor_tensor(out=smA[0:R, 40 + kv * 4:44 + kv * 4], in0=ov[:, :, 64], in1=esk[0:R, kv * 4:kv * 4 + 4], op=ALU.add),
                     reads=[rb, resk], writes=[rb, rsmA])
            P.op("dve", lambda e: e.reciprocal(out=smA[0:R, 48:56], in_=smA[0:R, 40:48]), reads=[rsmA], writes=[rsmA])
            P.op("dve", lambda e: e.tensor_scalar(out=smA[0:R, 48:56], in0=smA[0:R, 48:56], scalar1=0.5, scalar2=None, op0=ALU.mult), reads=[rsmA], writes=[rsmA])
            P.op("dve", lambda e: e.tensor_tensor(out=Ga[0:R, :].rearrange("p (h d) -> p h d", d=64), in0=Ga[0:R, :].rearrange("p (h d) -> p h d", d=64),
                                                   in1=smA[0:R, 48:56, None].broadcast_to([R, 8, 64]), op=ALU.mult), reads=[rGa, rsmA], writes=[rGa])
            for kv in range(2):
                bt, rb = banks[kv]
                ov = bt[0:R, 0:260].rearrange("p (g d) -> p g d", d=65)
                P.op("dve", lambda e, ov=ov, kv=kv: e.tensor_tensor(out=ym[0:R, 512 + kv * 256:512 + (kv + 1) * 256].rearrange("p (g d) -> p g d", d=64),
                                                                    in0=ov[:, :, 0:64], in1=Ga[0:R, kv * 256:(kv + 1) * 256].rearrange("p (g d) -> p g d", d=64),
                                                                    op=ALU.mult), reads=[rb, rGa], writes=[rb, rym])

        def back(R, xt, rxt, gate_t, rgate_t, y_dram, gated=False):
            bt, bb, rb = gbank()
            for j in range(8):
                P.op("pe", lambda e, j=j: e.transpose(out=bb[:, j * 128:j * 128 + R], in_=ym[0:R, j * 128:(j + 1) * 128], identity=ident[0:R, 0:R]),
                     reads=[rym, rcbf], writes=[rb], signal=(j == 7))
            P.op("act", lambda e: e.activation(out=TR[:, :, 0:R], in_=bb[:, :].rearrange("p (j t) -> p j t", t=128)[:, :, 0:R], func=AF.Copy),
                 reads=[rb], writes=[rb, rTR])
            yield
            for half in range(2):
                cs = slice(half * 512, (half + 1) * 512)
                btm, _, rbm = gbank()
                for kc in range(4):
                    P.op("pe", lambda e, kc=kc, btm=btm: e.matmul(btm[0:R, :], lhsT=TR[:, kc, 0:R], rhs=wmo[:, kc, cs], start=(kc == 0), stop=(kc == 3)),
                         reads=[rTR, rwmo], writes=[rbm], signal=(kc == 3))
                bta, _, rba = gbank()
                for kc in range(4):
                    P.op("pe", lambda e, kc=kc, bta=bta: e.matmul(bta[0:R, :], lhsT=TR[:, 4 + kc, 0:R], rhs=wao[:, kc, cs], start=(kc == 0), stop=(kc == 3)),
                         reads=[rTR, rwao], writes=[rba], signal=(kc == 3))
                P.op("dve", lambda e, btm=btm, half=half: e.scalar_tensor_tensor(out=u[0:R, half * 512:(half + 1) * 512], in0=thg[0:R, half, :], scalar=1.0, in1=btm[0:R, :],
                                                                                 op0=ALU.add, op1=ALU.mult), reads=[rthg, rbm], writes=[rbm, ru])
                P.op("dve", lambda e, bta=bta, half=half: e.scalar_tensor_tensor(out=t12[0:R, :], in0=thg[0:R, 2 + half, :], scalar=1.0, in1=bta[0:R, :],
                                                                                 op0=ALU.add, op1=ALU.mult), reads=[rthg, rba], writes=[rba, rt12])
                P.op("dve", lambda e, cs=cs: e.tensor_tensor(out=u[0:R, cs], in0=u[0:R, cs], in1=t12[0:R, :], op=ALU.add), reads=[rt12, ru], writes=[ru])
                yield
            bt, bb, rb = gbank()
            for j in range(8):
                P.op("pe", lambda e, j=j: e.transpose(out=bb[:, j * 128:j * 128 + R], in_=u[0:R, j * 128:(j + 1) * 128], identity=ident[0:R, 0:R]),
                     reads=[ru, rcbf], writes=[rb], signal=(j == 7))
            P.op("act", lambda e: e.activation(out=TR[:, :, 0:R], in_=bb[:, :].rearrange("p (j t) -> p j t", t=128)[:, :, 0:R], func=AF.Copy),
                 reads=[rb], writes=[rb, rTR])
            yield
            for half in range(2):
                cs = slice(half * 512, (half + 1) * 512)
                btf, _, rbf = gbank()
                for kc in range(8):
                    P.op("pe", lambda e, kc=kc, btf=btf: e.matmul(btf[0:R, :], lhsT=TR[:, kc, 0:R], rhs=wo[:, kc, cs], start=(kc == 0), stop=(kc == 7)),
                         reads=[rTR, rwo], writes=[rbf], signal=(kc == 7))
                if gated:
                    P.op("dve", lambda e, btf=btf, cs=cs: e.tensor_tensor(out=xt[0:R, cs], in0=btf[0:R, :], in1=xt[0:R, cs], op=ALU.add),
                         reads=[rbf, rxt], writes=[rbf, rxt])
                else:
                    P.op("dve", lambda e, btf=btf, cs=cs: e.tensor_tensor(out=t3[0:R, :], in0=btf[0:R, :], in1=gate_t[0:R, cs], op=ALU.mult),
                         reads=[rbf, rgate_t], writes=[rbf, rt3])
                    P.op("dve", lambda e, cs=cs: e.tensor_tensor(out=xt[0:R, cs], in0=xt[0:R, cs], in1=t3[0:R, :], op=ALU.add), reads=[rt3, rxt], writes=[rxt])
            yield
            act_accum(htmp[0:R, :], xt[0:R, 0:512], smB[0:R, 60:61], [rxt], [rhtmp, rsmB])
            P.op("dve", lambda e: e.tensor_scalar(out=smB[0:R, 62:63], in0=smB[0:R, 60:61], scalar1=1.0 / D, scalar2=EPS, op0=ALU.mult, op1=ALU.add),
                 reads=[rsmB], writes=[rsmB])
            act_accum(htmp[0:R, :], xt[0:R, 512:1024], smB[0:R, 61:62], [rxt], [rhtmp, rsmB])
            P.op("dve", lambda e: e.scalar_tensor_tensor(out=smB[0:R, 62:63], in0=smB[0:R, 61:62], scalar=1.0 / D, in1=smB[0:R, 62:63], op0=ALU.mult, op1=ALU.add),
                 reads=[rsmB], writes=[rsmB])
            P.op("pool", lambda e: e.tensor_tensor(out=smB[0:R, 63:64], in0=smB[0:R, 62:63], in1=nhalf[0:R, 0:1], op=ALU.pow), reads=[rsmB, rnhalf], writes=[rsmB])
            P.op("dve", lambda e: e.scalar_tensor_tensor(out=xt[0:R, :], in0=xt[0:R, :], scalar=smB[0:R, 63:64], in1=gfin[0:R, :], op0=ALU.mult, op1=ALU.mult),
                 reads=[rxt, rsmB, rgfin], writes=[rxt])
            P.op("sp", lambda e: e.dma_start(out=y_dram, in_=xt[0:R, :]), reads=[rxt], writes=[], dma=rxt)
            yield

        def gate_common(R):
            P.op("dve", lambda e: e.tensor_tensor(out=sm[0:R, 68:72], in0=G8[0:R, 0:4], in1=bif[0:R, 0:4], op=ALU.add), reads=[rG8, rbif], writes=[rsm])
            P.op("dve", lambda e: e.tensor_tensor(out=sm[0:R, 64:68], in0=G8[0:R, 4:8], in1=bif[0:R, 4:8], op=ALU.add), reads=[rG8, rbif], writes=[rsm])
            P.op("act", lambda e: e.activation(out=sm[0:R, 64:68], in_=sm[0:R, 64:68], func=AF.Exp, scale=-1.0), reads=[rsm], writes=[rsm])
            P.op("act", lambda e: e.activation(out=sm[0:R, 64:68], in_=sm[0:R, 64:68], func=AF.Ln, bias=1.0), reads=[rsm], writes=[rsm])
            P.op("dve", lambda e: e.tensor_scalar(out=sm[0:R, 64:68], in0=sm[0:R, 64:68], scalar1=-1.0, scalar2=None, op0=ALU.mult), reads=[rsm], writes=[rsm])


        def sample_tile(ss):
            R = 64
            ncs = [0]

            def TS(shape, dt, name):
                ncs[0] += 1
                nm = "s%s_%d" % (name, ncs[0])
                return ss.enter_context(nc.sbuf_tensor(nm, list(shape), dt)), Res(nm)
            xt, rxt = xs2[0]
            for _ in front(R, xs_d, xt, rxt, True, cst[0:64, C_RS:C_RS + 16], vaug[0][0], vaug[0][1]):
                pass
            for _ in zgates(R):
                pass
            stm_t, rstm = TS([16, 4], F32, "stm")
            msn, rmsn = TS([64, 4], F32, "msn")
            Wd, rWd = TS([64, 16, 4], F32, "Wd")
            wCb, rwCb = TS([128, 16, 4], F32, "wCb")
            nS, rnS = TS([128, 16, 4], F32, "nS")
            nld, rnld = TS([64, 128], F32, "nld")
            Ch = [TS([128, 128], F32, "Ch%d" % i) for i in range(NRING)]
            Cb = [TS([128, 128], BF16, "Cb%d" % i) for i in range(NRING)]
            nb1, rnb1 = TS([128, 16, 4], BF16, "nb1")
            qm = [TS([128, 64], BF16, "qm%d" % i) for i in range(4)]
            kwb = [TS([64, 128], BF16, "kwb%d" % i) for i in range(4)]
            ckr = [TS([128, 128], BF16, "ck%d" % i) for i in range(6)]
            ckTr = [TS([128, 128], BF16, "ckT%d" % i) for i in range(2)]
            cvr = [TS([128, 2, 66], BF16, "cv%d" % i) for i in range(6)]
            PF = [TS([128, 8, 64], BF16, "PF%d" % i) for i in range(2)]
            pcx = [TS([128, 32], BF16, "pcx%d" % i) for i in range(2)]
            P.op("sp", lambda e: e.dma_start(out=stm_t[:], in_=stm_d), writes=[rstm], dma=rstm)
            P.op("sp", lambda e: e.dma_start(out=nld[:], in_=stn_d), writes=[rnld], dma=rnld)
            for (t_, r_) in qm + PF:
                P.op("pool", lambda e, t_=t_: e.memset(t_[:], 0.0), writes=[r_])
            for (t_, r_) in cvr:
                P.op("pool", lambda e, t_=t_: e.memset(t_[:], 1.0), writes=[r_])
            gate_common(R)
            bt, bb, rb = gbank()
            P.op("pe", lambda e: e.matmul(bt[0:64, 0:4], lhsT=cst[0:64, C_BC:C_BC + 64], rhs=sm[0:64, 64:68], start=True, stop=True), reads=[rcst, rsm], writes=[rb])
            P.op("pe", lambda e: e.matmul(bt[0:64, 4:8], lhsT=cst[0:64, C_BS:C_BS + 64], rhs=sm[0:64, 64:68], start=True, stop=True), reads=[rcst, rsm], writes=[rb])
            P.op("dve", lambda e: e.tensor_copy(out=sm[0:64, 72:80], in_=bt[0:64, 0:8]), reads=[rb], writes=[rb, rsm])
            P.op("dve", lambda e: e.tensor_tensor(out=sm[0:64, 80:84], in0=sm[0:64, 68:72], in1=sm[0:64, 72:76], op=ALU.subtract), reads=[rsm], writes=[rsm])
            bt, bb, rb = gbank()
            for k, off in enumerate((C_P1, C_P2, C_P3)):
                P.op("pe", lambda e, k=k, off=off: e.matmul(bt[0:64, k * 4:k * 4 + 4], lhsT=cst[0:64, off:off + 64], rhs=sm[0:64, 80:84], start=True, stop=True),
                     reads=[rcst, rsm], writes=[rb])
            P.op("pe", lambda e: e.matmul(bt[0:64, 12:16], lhsT=cst[0:16, C_BEXP:C_BEXP + 64], rhs=stm_t[:], start=True, stop=True), reads=[rcst, rstm], writes=[rb])
            P.op("dve", lambda e: e.tensor_copy(out=sm[0:64, 84:100 - 4], in_=bt[0:64, 0:12]), reads=[rb], writes=[rb, rsm])
            P.op("dve", lambda e: e.tensor_copy(out=E12[0:64, 0:4], in_=bt[0:64, 12:16]), reads=[rb], writes=[rb, rE12])
            P.op("dve", lambda e: e.tensor_tensor(out=sm[0:64, 84:88], in0=sm[0:64, 84:88], in1=sm[0:64, 88:92], op=ALU.max), reads=[rsm], writes=[rsm])
            P.op("dve", lambda e: e.tensor_tensor(out=sm[0:64, 84:88], in0=sm[0:64, 84:88], in1=sm[0:64, 92:96], op=ALU.max), reads=[rsm], writes=[rsm])
            P.op("dve", lambda e: e.tensor_tensor(out=sm[0:64, 84:88], in0=sm[0:64, 84:88], in1=sm[0:64, 80:84], op=ALU.max), reads=[rsm], writes=[rsm])
            P.op("dve", lambda e: e.tensor_tensor(out=sm[0:64, 92:96], in0=sm[0:64, 84:88], in1=E12[0:64, 0:4], op=ALU.max), reads=[rsm, rE12], writes=[rsm])
            P.op("dve", lambda e: e.tensor_tensor(out=E12[0:64, 0:4], in0=E12[0:64, 0:4], in1=sm[0:64, 92:96], op=ALU.subtract), reads=[rsm, rE12], writes=[rE12])
            P.op("dve", lambda e: e.tensor_tensor(out=E12[0:64, 4:8], in0=sm[0:64, 80:84], in1=sm[0:64, 92:96], op=ALU.subtract), reads=[rsm], writes=[rE12])
            P.op("dve", lambda e: e.scalar_tensor_tensor(out=E12[0:64, 8:12], in0=sm[0:64, 72:76], scalar=-1.0, in1=sm[0:64, 92:96], op0=ALU.mult, op1=ALU.subtract),
                 reads=[rsm], writes=[rE12])
            P.op("dve", lambda e: e.tensor_scalar(out=E12[0:64, 12:16], in0=E12[0:64, 8:12], scalar1=2.0, scalar2=LN_EPS, op0=ALU.mult, op1=ALU.add),
                 reads=[rE12], writes=[rE12])
            P.op("dve", lambda e: e.tensor_tensor(out=msn[:], in0=sm[0:64, 76:80], in1=sm[0:64, 92:96], op=ALU.add), reads=[rsm], writes=[rmsn])
            P.op("act", lambda e: e.activation(out=X12[0:64, :], in_=E12[0:64, :], func=AF.Exp), reads=[rE12], writes=[rX12])
            for b in range(16):
                P.op("sp", lambda e, b=b: e.dma_start(out=ms_d[b:b + 1, :], in_=msn[4 * b:4 * b + 1, :]), reads=[rmsn], dma=rmsn)
            wC = X12[0:64, 0:4]; w = X12[0:64, 4:8]; thr = X12[0:64, 12:16]
            P.op("dve", lambda e: e.tensor_tensor(out=Wd[:], in0=wC[:, None, :].broadcast_to([64, 16, 4]),
                                                  in1=cst[0:64, C_BM0:C_BM0 + 16, None].broadcast_to([64, 16, 4]), op=ALU.mult), reads=[rX12, rcst], writes=[rWd])
            bt, bb, rb = gbank()
            P.op("pe", lambda e: e.matmul(bt[:, 0:64], lhsT=onesf[0:64, :], rhs=Wd[:].rearrange("p b h -> p (b h)"), start=True, stop=True), reads=[rcst, rWd], writes=[rb])
            P.op("dve", lambda e: e.tensor_copy(out=wCb[:].rearrange("p b h -> p (b h)"), in_=bt[:, 0:64]), reads=[rb], writes=[rb, rwCb])
            bt, bb, rb = gbank()
            P.op("pe", lambda e: e.transpose(out=bt[:, 0:64], in_=nld[:], identity=identf[0:64, 0:64]), reads=[rnld, rcst], writes=[rb])
            P.op("dve", lambda e: e.tensor_copy(out=nS[:].rearrange("p b h -> p (b h)"), in_=bt[:, 0:64]), reads=[rb], writes=[rb, rnS])
            P.op("dve", lambda e: e.tensor_tensor(out=nS[:], in0=nS[:], in1=wCb[:], op=ALU.mult), reads=[rnS, rwCb], writes=[rnS])
            P.op("dve", lambda e: e.tensor_copy(out=nb1[:], in_=nS[:]), reads=[rnS], writes=[rnb1])
            obanks = []

            def s_m():
                mlstm_transposes(R, pb[3])
                P.op("dve", lambda e: e.tensor_tensor(out=kw[0:64, :].rearrange("p (h d) -> p h d", d=128), in0=qkv[0:64, 1, :].rearrange("p (h d) -> p h d", d=128),
                                                      in1=w[:, :, None].broadcast_to([64, 4, 128]), op=ALU.mult), reads=[rqkv, rX12], writes=[rkw])
                P.op("dve", lambda e: e.tensor_tensor(out=wm[0:64, :, 0:64], in0=cbf[0:64, None, 512:576].broadcast_to([64, 4, 64]),
                                                      in1=w[:, :, None].broadcast_to([64, 4, 64]), op=ALU.mult), reads=[rcbf, rX12], writes=[rwm])
                bs_t, _, rbs = pb[3]
                for h in range(4):
                    P.op("pe", lambda e, h=h: e.matmul(bs_t[0:64, h * 64:(h + 1) * 64], lhsT=TR[:, 4 + h, 0:64], rhs=TR[:, h, 0:64], start=True, stop=True),
                         reads=[rTR], writes=[rbs])
                P.op("dve", lambda e: e.tensor_tensor(out=Sp[0:64, :, 0:64], in0=bs_t[0:64, 0:256].rearrange("p (h t) -> p h t", t=64), in1=wm[0:64, :, 0:64], op=ALU.mult),
                     reads=[rbs, rwm], writes=[rbs, rSp])
                bn_t, _, rbn = pb[4]
                bd_t, _, rbd = pb[7]
                bnu_t, rbnu = pb[7][0][:, 8:72], pb[7][2]
                P.op("dve", lambda e: e.memset(bnu_t, 0.0), writes=[rbnu])
                it = 0

                def cload(k):
                    if k < 64:
                        h_, b_ = k // 16, k % 16
                        c_t_, rc__ = Ch[k % NRING]
                        P.op("act", lambda e: e.dma_start(out=c_t_[:], in_=stC_d[b_, h_]), writes=[rc__], dma=rc__)
                for k_ in range(PFD):
                    cload(k_)
                for h in range(4):
                    vh = qkv[0:64, 2, h * 128:(h + 1) * 128]
                    P.op("pe", lambda e, h=h, vh=vh: e.matmul(bn_t[0:64, h * 128:(h + 1) * 128], lhsT=Sp[0:64, h, 0:64], rhs=vh, start=True, stop=False, skip_group_check=True),
                         reads=[rSp, rqkv], writes=[rbn])
                    P.op("pe", lambda e, h=h: e.matmul(bd_t[0:64, h:h + 1], lhsT=Sp[0:64, h, 0:64], rhs=onesb[0:64, 0:1], start=True, stop=False, skip_group_check=True),
                         reads=[rSp, rcbf], writes=[rbd])
                    for b in range(16):
                        c_t, rc_ = Ch[it % NRING]; cb_t, rcb = Cb[it % NRING]; q_t, rq = qm[it % 4]; kb_t, rkb = kwb[it % 4]
                        cload(it + PFD)
                        it += 1
                        P.op("act", lambda e, c_t=c_t, cb_t=cb_t, b=b, h=h: e.activation(out=cb_t[:], in_=c_t[:], func=AF.Copy, scale=wCb[:, b, h:h + 1]),
                             reads=[rc_, rwCb], writes=[rcb])
                        P.op("pool", lambda e, q_t=q_t, b=b, h=h: e.tensor_copy(out=q_t[:, 4 * b:4 * b + 4], in_=TR[:, h, 4 * b:4 * b + 4]), reads=[rTR], writes=[rq])
                        P.op("pe", lambda e, q_t=q_t, cb_t=cb_t, h=h, b=b: e.matmul(bn_t[0:64, h * 128:(h + 1) * 128], lhsT=q_t[:, :], rhs=cb_t[:, :],
                                                                                   start=False, stop=(b == 15), skip_group_check=True), reads=[rq, rcb], writes=[rbn])
                        P.op("pe", lambda e, q_t=q_t, h=h, b=b: e.matmul(bd_t[0:64, h:h + 1], lhsT=q_t[:, :], rhs=nb1[:, b, h:h + 1],
                                                                         start=False, stop=(b == 15), skip_group_check=True), reads=[rq, rnb1], writes=[rbd])
                        P.op("pool", lambda e, q_t=q_t, b=b: e.memset(q_t[:, 4 * b:4 * b + 4], 0.0), writes=[rq])
                        P.op("dve", lambda e, kb_t=kb_t, b=b, h=h: e.tensor_scalar(out=kb_t[:], in0=kw[0:64, h * 128:(h + 1) * 128], scalar1=cst[0:64, C_BM + b:C_BM + b + 1],
                                                                                   scalar2=None, op0=ALU.mult), reads=[rkw, rcst], writes=[rkb])
                        bcu, _, rbcu = tbank()
                        P.op("pe", lambda e, kb_t=kb_t, bcu=bcu, vh=vh: e.matmul(bcu[:, 0:128], lhsT=kb_t[:, :], rhs=vh, start=True, stop=True), reads=[rkb, rqkv], writes=[rbcu])
                        P.op("pe", lambda e, kb_t=kb_t, b=b, h=h: e.matmul(bnu_t[:, b * 4 + h:b * 4 + h + 1], lhsT=kb_t[:, :], rhs=onesb[0:64, 0:1], start=False, stop=True, skip_group_check=True),
                             reads=[rkb, rcbf], writes=[rbnu])
                        P.op("dve", lambda e, c_t=c_t, bcu=bcu, b=b, h=h: e.scalar_tensor_tensor(out=c_t[:], in0=c_t[:], scalar=wCb[:, b, h:h + 1], in1=bcu[:, 0:128],
                                                                                               op0=ALU.mult, op1=ALU.add), reads=[rc_, rwCb, rbcu, rcb], writes=[rc_, rbcu])
                        P.op("sp", lambda e, c_t=c_t, b=b, h=h: e.dma_start(out=Cs_d[b, h], in_=c_t[:]), reads=[rc_], dma=rc_)
                        yield
                P.op("dve", lambda e: e.tensor_tensor(out=nS[:].rearrange("p b h -> p (b h)"), in0=nS[:].rearrange("p b h -> p (b h)"), in1=bnu_t[:, 0:64], op=ALU.add),
                     reads=[rnS, rbnu, rnb1], writes=[rnS, rbnu])
                bt, bb, rb = tbank()
                P.op("pe", lambda e: e.transpose(out=bt[0:64, 0:128], in_=nS[:].rearrange("p b h -> p (b h)"), identity=identf), reads=[rnS, rcst], writes=[rb])
                P.op("dve", lambda e: e.tensor_copy(out=nld[:], in_=bt[0:64, 0:128]), reads=[rb], writes=[rb, rnld])
                P.op("sp", lambda e: e.dma_start(out=ns_d, in_=nld[:]), reads=[rnld], dma=rnld)
                mlstm_finish(R, bn_t, bd_t[0:64, 0:4], rbn, rbd, thr)
                yield

            def s_a():
                ak_c, rak_c = akT[0]
                va_c, rva_c = vaug[0]
                attn_transposes(R, ak_c, rak_c, pb[5])
                yield
                rko = Res("ko")
                P.op("sp", lambda e: e.dma_start(out=ks_d[:, 0:124, :], in_=ck_d[:, 4:128, :]), dma=rko)
                P.op("sp", lambda e: e.dma_start(out=vs_d[:, 0:124, :], in_=cv_d[:, 4:128, :]), dma=rko)
                for b in range(16):
                    P.op("sp", lambda e, b=b: e.dma_start(out=ks_d[b, 124:128, :], in_=A[4 * b:4 * b + 4, 512:640]), reads=[rA], dma=rA)
                    P.op("sp", lambda e, b=b: e.dma_start(out=vs_d[b, 124:128, :], in_=A[4 * b:4 * b + 4, 640:768]), reads=[rA], dma=rA)
                for kv in range(2):
                    ps = slice(kv * 64, (kv + 1) * 64)
                    bsn, _, rbsn = pb[5 + kv]
                    for j in range(4):
                        P.op("pe", lambda e, kv=kv, j=j, ps=ps, bsn=bsn: e.matmul(bsn[0:64, j * 64:(j + 1) * 64], lhsT=ak_c[ps, 0:64], rhs=aqT[ps, j, 0:64],
                                                                                  start=True, stop=True), reads=[rak_c, raqT], writes=[rbsn])
                    P.op("act", lambda e, kv=kv, bsn=bsn: e.activation(out=Pp[0:64, kv * 256:(kv + 1) * 256], in_=bsn[0:64, 0:256], func=AF.Exp, scale=0.125),
                         reads=[rbsn], writes=[rbsn, rPp])
                P.op("pool", lambda e: e.tensor_tensor(out=Pp[0:64, :].rearrange("p (g t) -> p g t", t=64), in0=Pp[0:64, :].rearrange("p (g t) -> p g t", t=64),
                                                       in1=cbf[0:64, None, 512:576].broadcast_to([64, 8, 64]), op=ALU.mult), reads=[rPp, rcbf], writes=[rPp])
                yield
                obanks.extend([(pb[5][0], pb[5][2]), (pb[6][0], pb[6][2])])
                for kv in range(2):
                    bo, rbo = obanks[kv]
                    P.op("dve", lambda e, bo=bo: e.memset(bo[0:64, 0:260], 0.0), writes=[rbo])
                    for g in range(4):
                        P.op("pe", lambda e, bo=bo, g=g, kv=kv: e.matmul(bo[0:64, g * 65:(g + 1) * 65], lhsT=Pp[0:64, (kv * 4 + g) * 64:(kv * 4 + g + 1) * 64], rhs=va_c[0:64, kv, :],
                                                                         start=False, stop=False, skip_group_check=True), reads=[rPp, rva_c], writes=[rbo])
                def kload(k):
                    if k < 16:
                        ck_t_, rck_ = ckr[k % 6]
                        cv_t_, rcv_ = cvr[k % 6]
                        P.op("pool", lambda e: e.dma_start(out=ck_t_[:], in_=ck_d[k]), writes=[rck_], dma=rck_)
                        for kv_ in range(2):
                            P.op("pool", lambda e, kv_=kv_: e.dma_start(out=cv_t_[:, kv_, 0:64], in_=cv_d[k][:, kv_ * 64:(kv_ + 1) * 64]), writes=[rcv_], dma=rcv_)
                for k_ in range(PFD):
                    kload(k_)
                for b in range(16):
                    ck_t, rck = ckr[b % 6]; ckT_t, rckT = ckTr[b % 2]; cv_t, rcv = cvr[b % 6]; pf_t, rpf = PF[b % 2]; px_t, rpx = pcx[b % 2]
                    kload(b + PFD)
                    bt, bb, rb = tbank()
                    P.op("pe", lambda e, ck_t=ck_t, bb=bb: e.transpose(out=bb[:, 0:128], in_=ck_t[:], identity=ident), reads=[rck, rcbf], writes=[rb])
                    P.op("act", lambda e, ckT_t=ckT_t, bb=bb: e.activation(out=ckT_t[:], in_=bb[:, 0:128], func=AF.Copy), reads=[rb], writes=[rb, rckT])
                    for kv in range(2):
                        ps = slice(kv * 64, (kv + 1) * 64)
                        bsc, _, rbsc = tbank()
                        for j in range(4):
                            P.op("pe", lambda e, kv=kv, j=j, ps=ps, b=b, ckT_t=ckT_t, bsc=bsc: e.matmul(bsc[:, j * 4:(j + 1) * 4], lhsT=ckT_t[ps, :],
                                                                                                     rhs=aqT[ps, j, 4 * b:4 * b + 4], start=True, stop=True),
                                 reads=[rckT, raqT], writes=[rbsc])
                        P.op("act", lambda e, px_t=px_t, bsc=bsc, kv=kv: e.activation(out=px_t[:, kv * 16:(kv + 1) * 16], in_=bsc[:, 0:16], func=AF.Exp, scale=0.125),
                             reads=[rbsc], writes=[rbsc, rpx])
                    P.op("dve", lambda e, px_t=px_t, pf_t=pf_t, b=b: e.tensor_tensor(out=pf_t[:, :, 4 * b:4 * b + 4], in0=px_t[:].rearrange("p (h t) -> p h t", t=4),
                                                                                     in1=cbf[:, None, 256:260].broadcast_to([128, 8, 4]), op=ALU.mult),
                         reads=[rpx, rcbf], writes=[rpf])
                    for kv in range(2):
                        bo, rbo = obanks[kv]
                        for g in range(4):
                            P.op("pe", lambda e, bo=bo, g=g, kv=kv, pf_t=pf_t, cv_t=cv_t: e.matmul(bo[0:64, g * 65:(g + 1) * 65], lhsT=pf_t[:, kv * 4 + g, :], rhs=cv_t[:, kv, 0:65],
                                                                                                 start=False, stop=(b == 15), skip_group_check=True), reads=[rpf, rcv], writes=[rbo])
                    P.op("pool", lambda e, pf_t=pf_t, b=b: e.memset(pf_t[:, :, 4 * b:4 * b + 4], 0.0), writes=[rpf])
                    yield
                yield

            chains = [s_m(), s_a()]
            while chains:
                for g in list(chains):
                    try:
                        next(g)
                    except StopIteration:
                        chains.remove(g)
            attn_finish(R, obanks)
            for _ in back(R, xt, rxt, gatehs, rgatehs, ys_d):
                pass

        if RUN_SAMPLE:
            with contextlib.ExitStack() as ss:
                sample_tile(ss)
                P.barrier()

        for k_ in range(8):
            P.op("pool", lambda e, k_=k_: e.tensor_tensor(out=wo[:, k_, :], in0=wo[:, k_, :], in1=gateh[:, :], op=ALU.mult), reads=[rwo, rgateh], writes=[rwo])
        P.op("pool", lambda e: e.memset(Cst[:], 0.0), writes=[rCst])
        P.op("pool", lambda e: e.memset(nst[:], 0.0), writes=[rnst])
        P.op("pool", lambda e: e.memset(mprev[0][0][:], 0.0), writes=[mprev[0][1]])

        def prompt_front(i):
            xt, rxt = xs2[i % 2]
            va_t, rva_t = vaug[i % 3]
            return front(128, xp_d[i * 128:(i + 1) * 128, :], xt, rxt, False, cst[:, C_RP + i * 16:C_RP + (i + 1) * 16],
                         va_t, rva_t, last=(i == NT_RUN - 1))

        def mix(i):
            R = 128
            xt, rxt = xs2[i % 2]
            yield from zgates(R)
            chains = [mchain(i), achain(i)]
            while chains:
                for g in list(chains):
                    try:
                        next(g)
                    except StopIteration:
                        chains.remove(g)
                    yield
            if i == 0:
                dump("X12", X12[:], rX12); dump("ym", ym[:], rym); dump("sm", sm[:], rsm); dump("Cst", Cst[:], rCst)
            yield from back(R, xt, rxt, gateh, rgateh, yp_d[i * 128:(i + 1) * 128, :], gated=True)

        def mchain(i):
            R = 128
            mp_t, rmp = mprev[i % 2]
            mn_t, rmn = mprev[(i + 1) % 2]
            gate_common(R)
            bS, _, rbS = pb[7]
            bt, rb = bS, rbS
            P.op("pe", lambda e: e.matmul(bt[:, 8:12], lhsT=cst[:, C_LE:C_LE + 128], rhs=sm[:, 64:68], start=True, stop=True), reads=[rcst, rsm], writes=[rb])
            P.op("pe", lambda e: e.matmul(bt[:, 12:16], lhsT=onesf, rhs=sm[:, 64:68], start=True, stop=True), reads=[rcst, rsm], writes=[rb])
            P.op("dve", lambda e: e.tensor_copy(out=sm[:, 72:80], in_=bt[:, 8:16]), reads=[rb], writes=[rb, rsm])
            P.op("dve", lambda e: e.tensor_tensor(out=sm[:, 80:84], in0=sm[:, 68:72], in1=sm[:, 72:76], op=ALU.subtract), reads=[rsm], writes=[rsm])
            P.op("pe", lambda e: e.transpose(out=bt[0:4, 16:144], in_=sm[:, 80:84], identity=identf), reads=[rsm, rcst], writes=[rb])
            P.op("dve", lambda e: e.tensor_reduce(out=sm[0:4, 84:85], in_=bt[0:4, 16:144], axis=AX.X, op=ALU.max), reads=[rb], writes=[rb, rsm])
            P.op("dve", lambda e: e.tensor_scalar(out=sm[0:4, 88:92], in0=identf[0:4, 0:4], scalar1=sm[0:4, 84:85], scalar2=None, op0=ALU.mult),
                 reads=[rsm, rcst], writes=[rsm])
            P.op("pe", lambda e: e.matmul(bt[:, 144:148], lhsT=onesf[0:4, :], rhs=sm[0:4, 88:92], start=True, stop=True), reads=[rcst, rsm], writes=[rb])
            P.op("dve", lambda e: e.tensor_tensor(out=sm[:, 92:96], in0=bt[:, 144:148], in1=mp_t[:], op=ALU.max), reads=[rb, rmp], writes=[rb, rsm])
            P.op("dve", lambda e: e.tensor_tensor(out=E12[:, 0:4], in0=mp_t[:], in1=sm[:, 92:96], op=ALU.subtract), reads=[rmp, rsm], writes=[rE12])
            P.op("dve", lambda e: e.tensor_tensor(out=E12[:, 4:8], in0=sm[:, 80:84], in1=sm[:, 92:96], op=ALU.subtract), reads=[rsm], writes=[rE12])
            P.op("dve", lambda e: e.scalar_tensor_tensor(out=E12[:, 8:12], in0=sm[:, 72:76], scalar=-1.0, in1=sm[:, 92:96], op0=ALU.mult, op1=ALU.subtract),
                 reads=[rsm], writes=[rE12])
            P.op("dve", lambda e: e.tensor_scalar(out=E12[:, 12:16], in0=E12[:, 8:12], scalar1=2.0, scalar2=LN_EPS, op0=ALU.mult, op1=ALU.add),
                 reads=[rE12], writes=[rE12])
            P.op("dve", lambda e: e.tensor_tensor(out=mn_t[:], in0=sm[:, 76:80], in1=sm[:, 92:96], op=ALU.add), reads=[rsm], writes=[rmn])
            P.op("act", lambda e: e.activation(out=X12[:], in_=E12[:], func=AF.Exp), reads=[rE12], writes=[rX12])
            wC = X12[:, 0:4]; w = X12[:, 4:8]; thr = X12[:, 12:16]
            yield
            mlstm_transposes(R, pb[3])
            yield
            P.op("dve", lambda e: e.tensor_tensor(out=kw[:].rearrange("p (h d) -> p h d", d=128), in0=qkv[:, 1, :].rearrange("p (h d) -> p h d", d=128),
                                                  in1=w[:, :, None].broadcast_to([128, 4, 128]), op=ALU.mult), reads=[rqkv, rX12], writes=[rkw])
            P.op("dve", lambda e: e.tensor_tensor(out=Csb[:], in0=Cst[:], in1=wC[:, :, None].broadcast_to([128, 4, 128]), op=ALU.mult),
                 reads=[rCst, rX12], writes=[rCsb])
            P.op("dve", lambda e: e.tensor_tensor(out=nsb[:], in0=nst[:], in1=wC, op=ALU.mult), reads=[rnst, rX12], writes=[rnsb])
            P.op("dve", lambda e: e.tensor_tensor(out=wm[:], in0=cbf[:, None, 128:256].broadcast_to([128, 4, 128]),
                                                  in1=w[:, :, None].broadcast_to([128, 4, 128]), op=ALU.mult), reads=[rcbf, rX12], writes=[rwm])
            yield
            bs_t, _, rbs = pb[3]
            for h in range(4):
                P.op("pe", lambda e, h=h: e.matmul(bs_t[:, h * 128:(h + 1) * 128], lhsT=TR[:, 4 + h, :], rhs=TR[:, h, :], start=True, stop=True),
                     reads=[rTR], writes=[rbs], signal=(h == 3))
            P.op("dve", lambda e: e.tensor_tensor(out=Sp[:], in0=bs_t[:, :].rearrange("p (h t) -> p h t", t=128), in1=wm[:], op=ALU.mult),
                 reads=[rbs, rwm], writes=[rbs, rSp])
            yield
            bn_t, _, rbn = pb[4]
            bd_t, _, rbd = pb[7]
            bc_t, _, rbc = pb[3]
            for h in range(4):
                P.op("pe", lambda e, h=h: e.matmul(bn_t[:, h * 128:(h + 1) * 128], lhsT=Sp[:, h, :], rhs=qkv[:, 2, h * 128:(h + 1) * 128], start=True, stop=False),
                     reads=[rSp, rqkv], writes=[rbn], signal=False)
                P.op("pe", lambda e, h=h: e.matmul(bn_t[:, h * 128:(h + 1) * 128], lhsT=TR[:, h, :], rhs=Csb[:, h, :], start=False, stop=True),
                     reads=[rTR, rCsb], writes=[rbn], signal=(h == 3))
            for h in range(4):
                P.op("pe", lambda e, h=h: e.matmul(bd_t[:, h:h + 1], lhsT=Sp[:, h, :], rhs=onesb[:, 0:1], start=True, stop=False), reads=[rSp, rcbf], writes=[rbd], signal=False)
                P.op("pe", lambda e, h=h: e.matmul(bd_t[:, h:h + 1], lhsT=TR[:, h, :], rhs=nsb[:, h:h + 1], start=False, stop=True), reads=[rTR, rnsb], writes=[rbd], signal=False)
            for h in range(4):
                P.op("pe", lambda e, h=h: e.matmul(bd_t[:, 4 + h:5 + h], lhsT=kw[:, h * 128:(h + 1) * 128], rhs=onesb[:, 0:1], start=True, stop=True),
                     reads=[rkw, rcbf], writes=[rbd], signal=(h == 3))
            for h in range(4):
                P.op("pe", lambda e, h=h: e.matmul(bc_t[:, h * 128:(h + 1) * 128], lhsT=kw[:, h * 128:(h + 1) * 128], rhs=qkv[:, 2, h * 128:(h + 1) * 128],
                                                   start=True, stop=True), reads=[rkw, rqkv], writes=[rbc], signal=(h == 3))
            yield
            P.op("dve", lambda e: e.tensor_tensor(out=nst[:], in0=nst[:], in1=wC, op=ALU.mult), reads=[rnst, rX12, rnsb], writes=[rnst])
            P.op("dve", lambda e: e.tensor_tensor(out=nst[:], in0=nst[:], in1=bd_t[:, 4:8], op=ALU.add), reads=[rnst, rbd], writes=[rnst, rbd])
            for h in range(4):
                P.op("dve", lambda e, h=h: e.scalar_tensor_tensor(out=Cst[:, h, :], in0=Cst[:, h, :], scalar=wC[:, h:h + 1], in1=bc_t[:, h * 128:(h + 1) * 128],
                                                                  op0=ALU.mult, op1=ALU.add), reads=[rCst, rX12, rbc, rCsb], writes=[rCst, rbc])
            yield
            mlstm_finish(R, bn_t, bd_t[:, 0:4], rbn, rbd, thr)
            yield

        def achain(i):
            R = 128
            ak_c, rak_c = akT[i % 2]
            ak_p, rak_p = akT[(i + 1) % 2]
            va_c, rva_c = vaug[i % 3]
            va_p, rva_p = vaug[(i - 1) % 3]
            attn_transposes(R, ak_c, rak_c, pb[6])
            yield
            obanks = []
            for kv in range(2):
                ps = slice(kv * 64, (kv + 1) * 64)
                bsc, _, rbsc = pb[5]
                P.op("pe", lambda e, bsc=bsc, ps=ps: e.matmul(bsc[:, :], lhsT=ak_c[ps, :], rhs=aqT[ps, :, :], start=True, stop=True), reads=[rak_c, raqT], writes=[rbsc])
                P.op("act", lambda e, bsc=bsc: e.activation(out=Pc[:], in_=bsc[:, :], func=AF.Exp, scale=0.125), reads=[rbsc], writes=[rbsc, rPc])
                P.op("dve", lambda e: e.tensor_tensor(out=Pc[:].rearrange("p (g t) -> p g t", t=128), in0=Pc[:].rearrange("p (g t) -> p g t", t=128),
                                                       in1=cbf[:, None, 128:256].broadcast_to([128, 4, 128]), op=ALU.mult), reads=[rPc, rcbf], writes=[rPc])
                if i > 0:
                    bsp, _, rbsp = pb[5]
                    P.op("pe", lambda e, bsp=bsp, ps=ps: e.matmul(bsp[:, :], lhsT=ak_p[ps, :], rhs=aqT[ps, :, :], start=True, stop=True), reads=[rak_p, raqT], writes=[rbsp])
                    P.op("act", lambda e, bsp=bsp: e.activation(out=Pp[:], in_=bsp[:, :], func=AF.Exp, scale=0.125), reads=[rbsp], writes=[rbsp, rPp])
                    P.op("dve", lambda e: e.tensor_tensor(out=Pp[:].rearrange("p (g t) -> p g t", t=128), in0=Pp[:].rearrange("p (g t) -> p g t", t=128),
                                                           in1=cbf[:, None, 256:384].broadcast_to([128, 4, 128]), op=ALU.mult), reads=[rPp, rcbf], writes=[rPp])
                yield
                bo, rbo = (pb[6][0][:, 0:260], pb[6][2]) if kv == 0 else (pb[7][0][:, 200:460], pb[7][2])
                for g in range(4):
                    if i > 0:
                        P.op("pe", lambda e, g=g, bo=bo, kv=kv: e.matmul(bo[:, g * 65:(g + 1) * 65], lhsT=Pp[:, g * 128:(g + 1) * 128], rhs=va_p[:, kv, :],
                                                                         start=True, stop=False), reads=[rPp, rva_p], writes=[rbo], signal=False)
                    P.op("pe", lambda e, g=g, bo=bo, kv=kv: e.matmul(bo[:, g * 65:(g + 1) * 65], lhsT=Pc[:, g * 128:(g + 1) * 128], rhs=va_c[:, kv, :],
                                                                     start=(i == 0), stop=True), reads=[rPc, rva_c], writes=[rbo], signal=(g == 3))
                obanks.append((bo, rbo))
                yield
            attn_finish(R, obanks)
            yield

        alloc_fset()

        def step(g, p):
            use_set(p)
            try:
                next(g)
                return True
            except StopIteration:
                return False

        g0 = prompt_front(0)
        while step(g0, 0):
            pass
        for i in range(NT_RUN):
            gm_ = mix(i)
            gf_ = prompt_front(i + 1) if i + 1 < NT_RUN else None
            am, af = True, gf_ is not None
            while am or af:
                for _r in range(MIXSTEPS):
                    if am:
                        am = step(gm_, i % 2)
                for _r in range(FSTEPS):
                    if af:
                        af = step(gf_, (i + 1) % 2)
        use_set(0)
        P.op("sp", lambda e: e.dma_start(out=Cp_d.rearrange("h d e -> d h e"), in_=Cst[:]), reads=[rCst], dma=rCst)
        bt, bb, rb = gbank()
        P.op("pe", lambda e: e.transpose(out=bt[0:4, 0:128], in_=nst[:], identity=identf), reads=[rnst, rcst], writes=[rb])
        nT, rnT = htmp[0:4, 0:128], rhtmp
        P.op("dve", lambda e: e.tensor_copy(out=nT, in_=bt[0:4, 0:128]), reads=[rb], writes=[rb, rnT])
        P.op("sp", lambda e: e.dma_start(out=np_d, in_=nT), reads=[rnT], dma=rnT)
        mfin, rmfin = mprev[NT_RUN % 2]
        P.op("sp", lambda e: e.dma_start(out=mp_d, in_=mfin[0:1, :]), reads=[rmfin], dma=rmfin)

        P.barrier()
    return nc


_CACHE = {}


def kernel(x_prompt, x_sample, state_C, state_n, state_m, cache_k, cache_v, c_prompt, c_sample,
           w_ada, b_ada, g_norm, w_in, b_igate, b_fgate, g_mnorm, sinks, w_m_out, w_a_out, w_out, g_final):
    f = lambda a: np.ascontiguousarray(np.asarray(a, dtype=np.float32))
    if "nc" not in _CACHE:
        _CACHE["nc"] = build()
        _CACHE["cst"] = make_consts()
    nc = _CACHE["nc"]
    x_prompt = f(x_prompt); x_sample = f(x_sample); state_C = f(state_C); state_n = f(state_n); state_m = f(state_m)
    cache_k = f(cache_k); cache_v = f(cache_v); c_prompt = f(c_prompt); c_sample = f(c_sample)
    shared = {
        "w_ada": f(w_ada)[0], "b_ada": f(b_ada), "g_norm": f(g_norm), "w_in": f(w_in)[0], "b_i": f(b_igate), "b_f": f(b_fgate),
        "g_mn": f(g_mnorm), "sinks": f(sinks), "w_mo": f(w_m_out)[0], "w_ao": f(w_a_out)[0], "w_o": f(w_out)[0],
        "g_final": f(g_final).reshape(1, D), "cst": _CACHE["cst"],
    }
    in_maps = []
    for c in range(NCORES):
        b0, b1 = 16 * c, 16 * (c + 1)
        m = dict(shared)
        m["xp"] = x_prompt[c]
        m["xs"] = x_sample[b0:b1].reshape(64, D)
        m["cc"] = np.concatenate([c_prompt[c:c + 1], c_sample[b0:b1]], axis=0)
        m["stC"] = state_C[0, b0:b1]
        m["stn"] = state_n[0, b0:b1].reshape(64, 128)
        m["stm"] = state_m[0, b0:b1]
        m["ck"] = cache_k[0, b0:b1].reshape(16, 128, 128)
        m["cv"] = cache_v[0, b0:b1].reshape(16, 128, 128)
        in_maps.append(m)
    res = run_bass_kernel_spmd(nc, in_maps, core_ids=list(range(NCORES)), **({"trace": True} if TRACE else {}))
    if TRACE:
        _CACHE["exec_ns"] = res.exec_time_ns
    r = res.results
    if DEBUG:
        _CACHE["dbg"] = {k: r[0][k] for k in DBG_NAMES}
    cat = lambda k: np.stack([r[c][k] for c in range(NCORES)], axis=0)
    y_prompt = cat("yp")
    y_sample = cat("ys").reshape(128, 4, D)
    C_p = cat("Cp")[None]
    n_p = cat("np")[None]
    m_p = cat("mp").reshape(1, 8, 4)
    k_p = cat("kp").reshape(1, 8, 128, 2, 64)
    v_p = cat("vp").reshape(1, 8, 128, 2, 64)
    C_s = cat("Cs").reshape(1, 128, 4, 128, 128)
    n_s = cat("ns").reshape(1, 128, 4, 128)
    m_s = cat("ms").reshape(1, 128, 4)
    k_s = cat("ks").reshape(1, 128, 128, 2, 64)
    v_s = cat("vs").reshape(1, 128, 128, 2, 64)
    return (y_prompt, y_sample, C_p, n_p, m_p, k_p, v_p, C_s, n_s, m_s, k_s, v_s)
```

```python
import contextlib
import numpy as np
import concourse.bass as bass
import concourse.mybir as mybir
from concourse.bass_utils import run_bass_kernel_spmd

F32 = mybir.dt.float32
BF16 = mybir.dt.bfloat16
AF = mybir.ActivationFunctionType
ALU = mybir.AluOpType
AX = mybir.AxisListType

NCORES = 8
DEBUG = False
NT_RUN = 16
RUN_SAMPLE = True
SELF_SYNC_ENGS = ("act", "dve", "pool")
TRACE = False
SAMPLE_STAGE = 9
NRING = 9
PFD = 4
MIXSTEPS = 2
FSTEPS = 1
DBG_NAMES = []
D = 1024
SEQ = 2048
NT = SEQ // 128
NCOL = 5896
EPS = 1e-6
LN_EPS = -13.815510557964274
PAST = 16384
O_MQ, O_MK, O_MV, O_IG, O_MO, O_MZ, O_AQ, O_AK, O_AZ, O_GM, O_GA = 0, 512, 1024, 1536, 1544, 2056, 2568, 3080, 3336, 3848, 4872

C_ID, C_LE, C_GT, C_ONE, C_BC, C_BS, C_P1, C_P2, C_P3, C_BM, C_BM0, C_SEL, C_BEXP, C_RP, C_RS = (
    0, 128, 256, 384, 512, 576, 640, 704, 768, 832, 848, 864, 1056, 1120, 1376)
CW = 1392


class Res:
    __slots__ = ("name", "lw", "rd", "sem", "dcount")

    def __init__(self, name):
        self.name = name
        self.lw = None
        self.rd = []
        self.sem = None
        self.dcount = 0


class Prog:
    ENG = {"pe": "tensor", "act": "scalar", "dve": "vector", "pool": "gpsimd", "sp": "sync"}

    def __init__(self, nc, stack):
        self.nc = nc
        self.stack = stack
        self.cnt = {e: 0 for e in self.ENG}
        self.esem = {e: stack.enter_context(nc.semaphore("es_" + e)) for e in self.ENG if e != "sp"}
        self.waited = {e: {} for e in self.ENG}
        self.dres = []

    def op(self, eng, fn, reads=(), writes=(), dma=None, signal=True):
        deps = []
        for r in reads:
            if r.lw is not None:
                deps.append(r.lw)
        for w in writes:
            if w.lw is not None:
                deps.append(w.lw)
            deps.extend(w.rd)
        need = {}
        for tok in deps:
            if tok[0] == "e" and tok[1] == eng and (eng == "pe" or eng not in SELF_SYNC_ENGS):
                continue
            k = ("e", tok[1]) if tok[0] == "e" else ("d", id(tok[1]))
            if k not in need or need[k][2] < tok[2]:
                need[k] = tok
        e = getattr(self.nc, self.ENG[eng])
        for k, tok in need.items():
            if self.waited[eng].get(k, -1) >= tok[2]:
                continue
            self.waited[eng][k] = tok[2]
            e.wait_ge(self.esem[tok[1]] if tok[0] == "e" else tok[1].sem, tok[2])
        if dma is not None:
            if dma.sem is None:
                dma.sem = self.stack.enter_context(self.nc.semaphore("ds_" + dma.name))
                self.dres.append(dma)
            dma.dcount += 1
            tok = ("d", dma, dma.dcount * 16)
            fn(e).then_inc(dma.sem, 16)
        elif not signal:
            assert eng == "pe"
            tok = ("e", eng, self.cnt[eng] + 1)
            fn(e)
        else:
            self.cnt[eng] += 1
            tok = ("e", eng, self.cnt[eng])
            fn(e).then_inc(self.esem[eng], 1)
        for r in reads:
            r.rd.append(tok)
        for w in writes:
            w.lw = tok
            w.rd = []
        return tok

    def barrier(self, skip=()):
        for eng in self.ENG:
            e = getattr(self.nc, self.ENG[eng])
            for r in self.dres:
                if r in skip:
                    continue
                k = ("d", id(r))
                v = r.dcount * 16
                if v > 0 and self.waited[eng].get(k, -1) < v:
                    self.waited[eng][k] = v
                    e.wait_ge(r.sem, v)
            for en in ("pe", "act", "dve", "pool"):
                if en == eng:
                    continue
                k = ("e", en)
                v = self.cnt[en]
                if v > 0 and self.waited[eng].get(k, -1) < v:
                    self.waited[eng][k] = v
                    e.wait_ge(self.esem[en], v)


def make_consts():
    c = np.zeros((128, CW), np.float32)
    i = np.arange(128)
    c[:, C_ID:C_ID + 128] = np.eye(128, dtype=np.float32)
    c[:, C_LE:C_LE + 128] = (i[:, None] <= i[None, :])
    c[:, C_GT:C_GT + 128] = (i[:, None] > i[None, :])
    c[:, C_ONE:C_ONE + 128] = 1.0
    j = np.arange(64)
    same = (j[:, None] // 4 == j[None, :] // 4)
    c[:64, C_BC:C_BC + 64] = same & (j[:, None] <= j[None, :])
    c[:64, C_BS:C_BS + 64] = same
    for k, off in ((1, C_P1), (2, C_P2), (3, C_P3)):
        src = 4 * (j // 4) + (j + k) % 4
        c[src, off + j] = 1.0
    b = np.arange(16)
    c[:64, C_BM:C_BM + 16] = (j[:, None] // 4 == b[None, :])
    c[:64, C_BM0:C_BM0 + 16] = (j[:, None] == 4 * b[None, :])
    c[0, C_SEL:C_SEL + 128] = 1.0
    for bb in range(16):
        c[1 + bb, C_SEL + 128 + 4 * bb:C_SEL + 128 + 4 * bb + 4] = 1.0
        c[bb, C_BEXP + 4 * bb:C_BEXP + 4 * bb + 4] = 1.0
    inv = (500000.0 ** (-np.arange(0, 16, 2, dtype=np.float32) / 16)).astype(np.float32)
    pos = (np.arange(NT)[None, :] * 128 + i[:, None]).astype(np.float32)
    ang = pos[:, :, None] * inv[None, None, :]
    rp = np.concatenate([np.cos(ang), np.sin(ang)], axis=-1).astype(np.float32)
    c[:, C_RP:C_RP + NT * 16] = rp.reshape(128, NT * 16)
    poss = (PAST + (j % 4)).astype(np.float32)
    angs = poss[:, None] * inv[None, :]
    c[:64, C_RS:C_RS + 16] = np.concatenate([np.cos(angs), np.sin(angs)], axis=-1)
    return c


def build():
    nc = bass.Bass("TRN2", target_bir_lowering=False)

    def din(name, shape):
        return nc.dram_tensor(name, list(shape), F32, kind="ExternalInput").ap()

    def dout(name, shape):
        return nc.dram_tensor(name, list(shape), F32, kind="ExternalOutput").ap()

    xp_d = din("xp", [SEQ, D]); xs_d = din("xs", [64, D]); cc_d = din("cc", [17, D])
    stC_d = din("stC", [16, 4, 128, 128]); stn_d = din("stn", [64, 128]); stm_d = din("stm", [16, 4])
    ck_d = din("ck", [16, 128, 128]); cv_d = din("cv", [16, 128, 128])
    wada_d = din("w_ada", [D, 3 * D]); bada_d = din("b_ada", [1, 3 * D]); gn_d = din("g_norm", [1, D])
    win_d = din("w_in", [D, NCOL]); bi_d = din("b_i", [1, 4]); bf_d = din("b_f", [1, 4])
    gmn_d = din("g_mn", [1, 512]); sk_d = din("sinks", [1, 8])
    wmo_d = din("w_mo", [512, D]); wao_d = din("w_ao", [512, D]); wo_d = din("w_o", [D, D])
    gf_d = din("g_final", [1, D]); cst_d = din("cst", [128, CW])
    yp_d = dout("yp", [SEQ, D]); ys_d = dout("ys", [64, D])
    Cp_d = dout("Cp", [4, 128, 128]); np_d = dout("np", [4, 128]); mp_d = dout("mp", [1, 4])
    kp_d = dout("kp", [128, 128]); vp_d = dout("vp", [128, 128])
    Cs_d = dout("Cs", [16, 4, 128, 128]); ns_d = dout("ns", [64, 128]); ms_d = dout("ms", [16, 4])
    ks_d = dout("ks", [16, 128, 128]); vs_d = dout("vs", [16, 128, 128])

    with contextlib.ExitStack() as st:
        P = Prog(nc, st)
        ncnt = [0]
        dbg_list = []

        def dump(name, ap, res):
            if not DEBUG:
                return
            shp = [int(x) for x in ap.shape]
            dt_ = ap.dtype
            if dt_ != F32:
                return
            d = nc.dram_tensor("dbg_" + name, shp, dt_, kind="ExternalOutput").ap()
            rr = Res("dbg_" + name)
            P.op("sp", lambda e: e.dma_start(out=d, in_=ap), reads=[res], dma=rr)
            DBG_NAMES.append("dbg_" + name)

        def T(shape, dt, name=None):
            ncnt[0] += 1
            nm = (name or "t") + "_%d" % ncnt[0]
            return st.enter_context(nc.sbuf_tensor(nm, list(shape), dt)), Res(nm)

        def act_accum(out, in_, acc, reads, writes):
            P.op("act", lambda e: e.activation(out=out, in_=in_, func=AF.Square), reads=reads, writes=writes)
            P.op("dve", lambda e: e.tensor_reduce(out=acc, in_=out, axis=AX.X, op=ALU.add), reads=writes, writes=writes)

        Wb, rWb = T([128, 8, NCOL], BF16, "Wb")
        edges = [0, 512, 1024, 1536, 2056, 2568, 3080, 3336, 3848, 4360, 4872, 5384, 5896]
        rWbp = [Res("Wb%d" % j) for j in range(len(edges) - 1)]
        wmo, rwmo = T([128, 4, D], BF16, "wmo")
        wao, rwao = T([128, 4, D], BF16, "wao")
        wo, rwo = T([128, 8, D], BF16, "wo")
        cst, rcst = T([128, CW], F32, "cst")
        cbf, rcbf = T([128, 640], BF16, "cbf")
        gfin, rgfin = T([128, D], F32, "gfin")
        gmnq, rgmnq = T([128, 512], BF16, "gmnq")
        esk, resk = T([128, 8], F32, "esk")
        bif, rbif = T([128, 8], F32, "bif")
        nhalf, rnhalf = T([128, 8], F32, "nhalf")
        modT, rmodT = T([128, 24, 17], F32, "modT")
        GT, rGT = T([128, 8, 17], F32, "GT")
        gnT, rgnT = T([128, 8], F32, "gnT")
        badT, rbadT = T([128, 24], F32, "badT")
        gateh, rgateh = T([128, D], F32, "gateh")
        gatehs, rgatehs = T([64, D], BF16, "gatehs")
        pb = []
        for i in range(8):
            t_ = st.enter_context(nc.psum_tensor("pb%d" % i, [128, 512], F32))
            pb.append((t_, t_.bitcast(BF16), Res("pb%d" % i)))
        zb_i = [0]
        gb_i = [0]

        def zbank():
            zb_i[0] = (zb_i[0] + 1) % 2
            return pb[zb_i[0]]

        def gbank():
            gb_i[0] = (gb_i[0] + 1) % 5
            return pb[3 + gb_i[0]]

        def tbank():
            zb_i[0] = (zb_i[0] + 1) % 3
            return pb[zb_i[0]]

        ident = cbf[:, 0:128]
        identf = cst[:, C_ID:C_ID + 128]
        onesf = cst[:, C_ONE:C_ONE + 128]
        onesb = cbf[:, 384:512]

        P.op("sp", lambda e: e.dma_start(out=cst[:], in_=cst_d), writes=[rcst], dma=rcst)
        winv = win_d.rearrange("(k p) c -> p k c", p=128)
        P.op("sp", lambda e: e.dma_start(out=gfin[:], in_=gf_d.broadcast_to([128, D])), writes=[rgfin], dma=rgfin)
        P.op("pool", lambda e: e.dma_start(out=gmnq[:], in_=gmn_d.broadcast_to([128, 512])), writes=[rgmnq], dma=rgmnq)
        P.op("sp", lambda e: e.dma_start(out=esk[:], in_=sk_d.broadcast_to([128, 8])), writes=[resk], dma=resk)
        P.op("sp", lambda e: e.dma_start(out=bif[:, 0:4], in_=bi_d.broadcast_to([128, 4])), writes=[rbif], dma=rbif)
        P.op("sp", lambda e: e.dma_start(out=bif[:, 4:8], in_=bf_d.broadcast_to([128, 4])), writes=[rbif], dma=rbif)
        P.op("pool", lambda e: e.memset(nhalf[:], -0.5), writes=[rnhalf])
        P.op("dve", lambda e: e.tensor_copy(out=cbf[:], in_=cst[:, 0:640]), reads=[rcst], writes=[rcbf])
        P.op("dve", lambda e: e.tensor_scalar(out=gmnq[:], in0=gmnq[:], scalar1=0.25, scalar2=None, op0=ALU.mult),
             reads=[rgmnq], writes=[rgmnq])
        P.op("act", lambda e: e.activation(out=esk[:], in_=esk[:], func=AF.Exp), reads=[resk], writes=[resk])

        with contextlib.ExitStack() as s0:
            def T0(shape, dt, name):
                ncnt[0] += 1
                nm = name + "_%d" % ncnt[0]
                return s0.enter_context(nc.sbuf_tensor(nm, list(shape), dt)), Res(nm)
            c17, rc17 = T0([17, D], F32, "c17")
            c17t, rc17t = T0([17, D], F32, "c17t")
            c17b, rc17b = T0([17, D], BF16, "c17b")
            cT, rcT = T0([128, 8, 17], BF16, "cT")
            bgb, rbgb = T0([17, D], F32, "bgb")
            modg, rmodg = T0([17, D], F32, "modg")
            ld24, rld24 = T0([32, 128], F32, "ld24")
            was = [T0([128, 8, 512], BF16, "was%d" % i) for i in range(4)]
            P.op("sp", lambda e: e.dma_start(out=c17[:], in_=cc_d), writes=[rc17], dma=rc17)
            wadav = wada_d.rearrange("(k p) c -> p k c", p=128)
            for j in range(4):
                P.op("pool", lambda e, j=j: e.dma_start(out=was[j][0][:], in_=wadav[:, :, j * 512:(j + 1) * 512]), writes=[was[j][1]], dma=was[j][1])
            P.op("act", lambda e: e.activation(out=c17t[:], in_=c17[:], func=AF.Tanh, scale=0.5), reads=[rc17], writes=[rc17t])
            P.op("dve", lambda e: e.scalar_tensor_tensor(out=c17t[:], in0=c17t[:], scalar=1.0, in1=c17[:], op0=ALU.add, op1=ALU.mult),
                 reads=[rc17t, rc17], writes=[rc17t])
            P.op("dve", lambda e: e.tensor_scalar(out=c17b[:], in0=c17t[:], scalar1=0.5, scalar2=None, op0=ALU.mult),
                 reads=[rc17t], writes=[rc17b])
            bt, bb, rb = gbank()
            for k in range(8):
                P.op("pe", lambda e, k=k: e.transpose(out=bb[:, k * 32:k * 32 + 17], in_=c17b[:, k * 128:(k + 1) * 128], identity=ident[0:17, 0:17]),
                     reads=[rc17b, rcbf], writes=[rb])
            P.op("dve", lambda e: e.tensor_copy(out=cT[:], in_=bb[:, 0:256].rearrange("p (k r) -> p k r", r=32)[:, :, 0:17]),
                 reads=[rb], writes=[rb, rcT])
            P.op("sp", lambda e: e.dma_start(out=ld24[0:24, :], in_=bada_d.rearrange("o (j p) -> (o j) p", p=128)), writes=[rld24], dma=rld24)
            P.op("sp", lambda e: e.dma_start(out=ld24[24:32, :], in_=gn_d.rearrange("o (j p) -> (o j) p", p=128)), writes=[rld24], dma=rld24)
            bt, bb, rb = gbank()
            P.op("pe", lambda e: e.transpose(out=bt[:, 0:32], in_=ld24[:], identity=identf[0:32, 0:32]), reads=[rld24, rcst], writes=[rb])
            P.op("dve", lambda e: e.tensor_copy(out=badT[:], in_=bt[:, 0:24]), reads=[rb], writes=[rb, rbadT])
            P.op("dve", lambda e: e.tensor_copy(out=gnT[:], in_=bt[:, 24:32]), reads=[rb], writes=[rb, rgnT])
            P.op("sp", lambda e: e.dma_start(out=bgb[:], in_=bada_d[:, 2 * D:3 * D].broadcast_to([17, D])), writes=[rbgb], dma=rbgb)
            for j in range(6):
                wa, rwa = was[j % 4]
                if j >= 4:
                    P.op("pool", lambda e, wa=wa, j=j: e.dma_start(out=wa[:], in_=wadav[:, :, j * 512:(j + 1) * 512]), writes=[rwa], dma=rwa)
                if j < 4:
                    bt, bb, rb = gbank()
                    for sub in range(4):
                        for k in range(8):
                            P.op("pe", lambda e, wa=wa, sub=sub, k=k, bt=bt: e.matmul(
                                bt[:, sub * 32:sub * 32 + 17], lhsT=wa[:, k, sub * 128:(sub + 1) * 128], rhs=cT[:, k, :],
                                start=(k == 0), stop=(k == 7)), reads=[rwa, rcT], writes=[rb])
                    for sub in range(4):
                        jj = j * 4 + sub
                        P.op("dve", lambda e, bt=bt, sub=sub, jj=jj: e.tensor_scalar(
                            out=modT[:, jj, :], in0=bt[:, sub * 32:sub * 32 + 17], scalar1=badT[:, jj:jj + 1], scalar2=None, op0=ALU.add),
                            reads=[rb, rbadT], writes=[rb, rmodT])
                else:
                    bt, bb, rb = gbank()
                    for k in range(8):
                        P.op("pe", lambda e, wa=wa, k=k, bt=bt: e.matmul(bt[0:17, :], lhsT=cT[:, k, :], rhs=wa[:, k, :],
                                                                       start=(k == 0), stop=(k == 7)), reads=[rwa, rcT], writes=[rb])
                    P.op("dve", lambda e, bt=bt, j=j: e.tensor_tensor(out=modg[:, (j - 4) * 512:(j - 3) * 512], in0=bt[0:17, :],
                                                                     in1=bgb[:, (j - 4) * 512:(j - 3) * 512], op=ALU.add),
                         reads=[rb, rbgb], writes=[rb, rmodg])
            P.op("dve", lambda e: e.scalar_tensor_tensor(out=GT[:], in0=modT[:, 8:16, :], scalar=1.0,
                                                         in1=gnT[:, :, None].broadcast_to([128, 8, 17]), op0=ALU.add, op1=ALU.mult),
                 reads=[rmodT, rgnT], writes=[rGT])
            for half in range(2):
                bt, bb, rb = gbank()
                P.op("pe", lambda e, bt=bt, half=half: e.matmul(bt[:, :], lhsT=cst[0:17, C_SEL:C_SEL + 128], rhs=modg[:, half * 512:(half + 1) * 512],
                                                               start=True, stop=True), reads=[rcst, rmodg], writes=[rb])
                P.op("act", lambda e, bt=bt, half=half: e.activation(out=gateh[:, half * 512:(half + 1) * 512], in_=bt[:, :], func=AF.Copy, scale=0.5),
                     reads=[rb], writes=[rb, rgateh])
                bt, bb, rb = gbank()
                P.op("pe", lambda e, bt=bt, half=half: e.matmul(bt[0:64, :], lhsT=cst[0:17, C_SEL + 128:C_SEL + 192], rhs=modg[:, half * 512:(half + 1) * 512],
                                                               start=True, stop=True), reads=[rcst, rmodg], writes=[rb])
                P.op("act", lambda e, bt=bt, half=half: e.activation(out=gatehs[:, half * 512:(half + 1) * 512], in_=bt[0:64, :], func=AF.Copy, scale=0.5),
                     reads=[rb], writes=[rb, rgatehs])
            for j_, (a_, b_) in enumerate(zip(edges[:-1], edges[1:])):
                P.op("pool", lambda e, a_=a_, b_=b_: e.dma_start(out=Wb[:, :, a_:b_], in_=winv[:, :, a_:b_]), writes=[rWbp[j_]], dma=rWbp[j_])
            P.op("pool", lambda e: e.dma_start(out=wmo[:], in_=wmo_d.rearrange("(k p) c -> p k c", p=128)), writes=[rwmo], dma=rwmo)
            P.op("pool", lambda e: e.dma_start(out=wao[:], in_=wao_d.rearrange("(k p) c -> p k c", p=128)), writes=[rwao], dma=rwao)
            P.op("pool", lambda e: e.dma_start(out=wo[:], in_=wo_d.rearrange("(k p) c -> p k c", p=128)), writes=[rwo], dma=rwo)
            dump("modT", modT[:], rmodT); dump("GT", GT[:], rGT); dump("gateh", gateh[:], rgateh); dump("cT", cT[:], rcT)
            dump("modg", modg[:], rmodg); dump("badT", badT[:], rbadT); dump("gnT", gnT[:], rgnT)
            P.barrier(skip=rWbp + [rwmo, rwao, rwo])

        xs2 = [T([128, D], F32, "x%d" % i) for i in range(2)]
        xnb, rxnb = T([128, D], BF16, "xnb")
        htmp, rhtmp = T([128, 512], F32, "htmp")
        htmpF, rhtmpF = T([128, 512], F32, "htmpF")
        smF, rsmF = T([128, 8], F32, "smF")
        rtmpF, rrtmpF = htmpF[:, 0:320].rearrange("p (a h d) -> p a h d", a=4, h=10), rhtmpF
        tht, rth = T([128, 512], BF16, "tht")
        thg, rthg = T([128, 4, 512], BF16, "thg")
        t1z, rt1z = T([128, 512], BF16, "t1z")
        A, rA = T([128, 768], F32, "A")
        vaug = [T([128, 2, 65], BF16, "vaug%d" % i) for i in range(3)]
        fsets = []

        def alloc_fset():
            d = {}
            d["hT"], d["rhT"] = T([128, 8, 128], BF16, "hT")
            d["qkv"], d["rqkv"] = T([128, 3, 512], BF16, "qkv")
            d["G8"], d["rG8"] = T([128, 8], F32, "G8")
            d["Gm"], d["rGm"] = T([128, 512], BF16, "Gm")
            d["Ga"], d["rGa"] = T([128, 512], BF16, "Ga")
            d["qkb"], d["rqkb"] = T([128, 640], BF16, "qkb")
            fsets.append(d)
        alloc_fset()
        hT = rhT = qkv = rqkv = G8 = rG8 = Gm = rGm = Ga = rGa = qkb = rqkb = None

        def use_set(p):
            nonlocal hT, rhT, qkv, rqkv, G8, rG8, Gm, rGm, Ga, rGa, qkb, rqkb
            d = fsets[p]
            hT, rhT, qkv, rqkv, G8, rG8 = d["hT"], d["rhT"], d["qkv"], d["rqkv"], d["G8"], d["rG8"]
            Gm, rGm, Ga, rGa, qkb, rqkb = d["Gm"], d["rGm"], d["Ga"], d["rGa"], d["qkb"], d["rqkb"]
        use_set(0)
        akT = [T([128, 128], BF16, "akT%d" % i) for i in range(2)]
        aqT, raqT = T([128, 4, 128], BF16, "aqT")
        TR, rTR = T([128, 8, 128], BF16, "TR")
        kw, rkw = T([128, 512], BF16, "kw")
        Csb, rCsb = T([128, 4, 128], BF16, "Csb")
        nsb, rnsb = T([128, 4], BF16, "nsb")
        wm, rwm = T([128, 4, 128], BF16, "wm")
        Sp, rSp = T([128, 4, 128], BF16, "Sp")
        Cst, rCst = T([128, 4, 128], F32, "Cst")
        nst, rnst = T([128, 4], F32, "nst")
        ym, rym = T([128, 1024], BF16, "ym")
        Pc, rPc = T([128, 512], BF16, "Pc")
        Pp, rPp = T([128, 512], BF16, "Pp")
        t12, rt12 = T([128, 512], BF16, "t12")
        rtmp, rrtmp = rtmpF, rrtmpF
        u, ru = ym, rym
        t3, rt3 = htmp, rhtmp
        sm, rsm = T([128, 96], F32, "sm")
        smA, rsmA = T([128, 64], F32, "smA")
        smB, rsmB = T([128, 64], F32, "smB")
        mprev = [T([128, 4], F32, "mprev%d" % i) for i in range(2)]
        E12, rE12 = T([128, 16], F32, "E12")
        X12, rX12 = T([128, 16], F32, "X12")
        for (v_, rv_) in vaug:
            P.op("pool", lambda e, v_=v_: e.memset(v_[:], 1.0), writes=[rv_])

        def zmm(R, c0, c1):
            bt, bb, rb = zbank()
            rw = [rWbp[j] for j in range(len(edges) - 1) if edges[j] < c1 and edges[j + 1] > c0]
            for k in range(8):
                P.op("pe", lambda e, k=k, bt=bt: e.matmul(bt[0:R, 0:c1 - c0], lhsT=hT[:, k, 0:R], rhs=Wb[:, k, c0:c1],
                                                         start=(k == 0), stop=(k == 7)), reads=[rhT] + rw, writes=[rb], signal=(k == 7))
            return bt, rb

        def zgates(R):
            for j in range(4):
                bt, rb = zmm(R, O_GM + j * 512, O_GM + (j + 1) * 512)
                P.op("act", lambda e, bt=bt, j=j: e.activation(out=thg[0:R, j, :], in_=bt[0:R, :], func=AF.Tanh, scale=0.5), reads=[rb], writes=[rb, rthg])
                yield

        def front(R, x_ap_dram, xt, rxt, sample, ropeap, va_t=None, rva_t=None, last=False):
            P.op("act", lambda e: e.dma_start(out=xt[0:R, :], in_=x_ap_dram), writes=[rxt], dma=rxt)
            act_accum(htmpF[0:R, :], xt[0:R, 0:512], smF[0:R, 0:1], [rxt], [rhtmpF, rsmF])
            P.op("dve", lambda e: e.tensor_scalar(out=smF[0:R, 2:3], in0=smF[0:R, 0:1], scalar1=1.0 / D, scalar2=EPS, op0=ALU.mult, op1=ALU.add),
                 reads=[rsmF], writes=[rsmF])
            act_accum(htmpF[0:R, :], xt[0:R, 512:1024], smF[0:R, 1:2], [rxt], [rhtmpF, rsmF])
            P.op("dve", lambda e: e.scalar_tensor_tensor(out=smF[0:R, 2:3], in0=smF[0:R, 1:2], scalar=1.0 / D, in1=smF[0:R, 2:3], op0=ALU.mult, op1=ALU.add),
                 reads=[rsmF], writes=[rsmF])
            P.op("pool", lambda e: e.tensor_tensor(out=smF[0:R, 3:4], in0=smF[0:R, 2:3], in1=nhalf[0:R, 0:1], op=ALU.pow), reads=[rsmF, rnhalf], writes=[rsmF])
            P.op("act", lambda e: e.activation(out=xnb[0:R, :], in_=xt[0:R, :], func=AF.Copy, scale=smF[0:R, 3:4]), reads=[rxt, rsmF], writes=[rxnb])
            bt, bb, rb = pb[2]
            for k in range(8):
                P.op("pe", lambda e, k=k: e.transpose(out=bb[:, k * 128:k * 128 + R], in_=xnb[0:R, k * 128:(k + 1) * 128], identity=ident[0:R, 0:R]),
                     reads=[rxnb, rcbf], writes=[rb], signal=(k == 7))
            if not sample:
                for k in range(8):
                    P.op("act", lambda e, k=k: e.activation(out=hT[:, k, :], in_=bb[:, k * 128:(k + 1) * 128], func=AF.Identity,
                                                            scale=GT[:, k, 0:1], bias=modT[:, k, 0:1]), reads=[rb, rGT, rmodT], writes=[rb, rhT])
            else:
                src = bb[:, :].rearrange("p (k t) -> p k t", t=128)[:, :, 0:64].rearrange("p k (b t) -> p k b t", t=4)
                for k in range(8):
                    P.op("dve", lambda e, k=k: e.tensor_tensor(out=htmpF[:, 0:64].rearrange("p (b t) -> p b t", t=4), in0=src[:, k],
                                                               in1=GT[:, k, 1:17, None].broadcast_to([128, 16, 4]), op=ALU.mult),
                         reads=[rb, rGT], writes=[rb, rhtmpF])
                    P.op("dve", lambda e, k=k: e.tensor_tensor(out=hT[:, k, 0:64].rearrange("p (b t) -> p b t", t=4),
                                                               in0=htmpF[:, 0:64].rearrange("p (b t) -> p b t", t=4),
                                                               in1=modT[:, k, 1:17, None].broadcast_to([128, 16, 4]), op=ALU.add),
                         reads=[rhtmpF, rmodT], writes=[rhtmpF, rhT])

            yield
            bt, rb = zmm(R, O_IG, O_IG + 8)
            P.op("dve", lambda e, bt=bt: e.tensor_copy(out=G8[0:R, :], in_=bt[0:R, 0:8]), reads=[rb], writes=[rb, rG8])
            yield
            bt, rb = zmm(R, O_MQ, O_MQ + 512)
            P.op("act", lambda e, bt=bt: e.activation(out=qkv[0:R, 0, :], in_=bt[0:R, :], func=AF.Copy, scale=128.0 ** -0.5), reads=[rb], writes=[rb, rqkv])
            yield
            bt, rb = zmm(R, O_MK, O_MK + 512)
            P.op("dve", lambda e, bt=bt: e.tensor_copy(out=qkv[0:R, 1, :], in_=bt[0:R, :]), reads=[rb], writes=[rb, rqkv])
            yield
            bt, rb = zmm(R, O_MV, O_MV + 512)
            P.op("act", lambda e, bt=bt: e.activation(out=qkv[0:R, 2, :], in_=bt[0:R, :], func=AF.Copy), reads=[rb], writes=[rb, rqkv])
            yield
            bt, rb = zmm(R, O_AQ, O_AQ + 512)
            P.op("dve", lambda e, bt=bt: e.tensor_copy(out=A[0:R, 0:512], in_=bt[0:R, :]), reads=[rb], writes=[rb, rA])
            yield
            bt, rb = zmm(R, O_AK, O_AK + 256)
            P.op("act", lambda e, bt=bt: e.activation(out=A[0:R, 512:768], in_=bt[0:R, 0:256], func=AF.Copy), reads=[rb], writes=[rb, rA])
            yield
            bt, rb = zmm(R, O_MZ, O_MZ + 512)
            P.op("act", lambda e, bt=bt: e.activation(out=tht[0:R, :], in_=bt[0:R, :], func=AF.Tanh, scale=0.5), reads=[rb], writes=[rb, rth])
            P.op("dve", lambda e, bt=bt: e.scalar_tensor_tensor(out=t1z[0:R, :], in0=tht[0:R, :], scalar=1.0, in1=bt[0:R, :], op0=ALU.add, op1=ALU.mult),
                 reads=[rb, rth], writes=[rb, rt1z])
            yield
            bt, rb = zmm(R, O_MO, O_MO + 512)
            P.op("act", lambda e, bt=bt: e.activation(out=tht[0:R, :], in_=bt[0:R, :], func=AF.Tanh, scale=0.5), reads=[rb], writes=[rb, rth])
            P.op("dve", lambda e: e.scalar_tensor_tensor(out=Gm[0:R, :], in0=tht[0:R, :], scalar=1.0, in1=t1z[0:R, :], op0=ALU.add, op1=ALU.mult),
                 reads=[rth, rt1z], writes=[rGm])
            P.op("pool", lambda e: e.tensor_tensor(out=Gm[0:R, :], in0=Gm[0:R, :], in1=gmnq[0:R, :], op=ALU.mult), reads=[rGm, rgmnq], writes=[rGm])
            yield
            bt, rb = zmm(R, O_AZ, O_AZ + 512)
            P.op("act", lambda e, bt=bt: e.activation(out=tht[0:R, :], in_=bt[0:R, :], func=AF.Tanh, scale=0.5), reads=[rb], writes=[rb, rth])
            P.op("dve", lambda e, bt=bt: e.scalar_tensor_tensor(out=Ga[0:R, :], in0=tht[0:R, :], scalar=1.0, in1=bt[0:R, :], op0=ALU.add, op1=ALU.mult),
                 reads=[rb, rth], writes=[rb, rGa])
            yield
            Av = A[0:R, 0:640].rearrange("p (h d) -> p h d", d=64)
            cosb = ropeap[:, None, 0:8].broadcast_to([R, 10, 8])
            sinb = ropeap[:, None, 8:16].broadcast_to([R, 10, 8])
            P.op("pool", lambda e: e.tensor_tensor(out=rtmp[0:R, 0], in0=Av[:, :, 0:8], in1=cosb, op=ALU.mult), reads=[rA, rcst], writes=[rrtmp])
            P.op("pool", lambda e: e.tensor_tensor(out=rtmp[0:R, 1], in0=Av[:, :, 8:16], in1=sinb, op=ALU.mult), reads=[rA, rcst], writes=[rrtmp])
            P.op("pool", lambda e: e.tensor_tensor(out=rtmp[0:R, 2], in0=Av[:, :, 8:16], in1=cosb, op=ALU.mult), reads=[rA, rcst], writes=[rrtmp])
            P.op("pool", lambda e: e.tensor_tensor(out=rtmp[0:R, 3], in0=Av[:, :, 0:8], in1=sinb, op=ALU.mult), reads=[rA, rcst], writes=[rrtmp])
            P.op("pool", lambda e: e.tensor_tensor(out=Av[:, :, 0:8], in0=rtmp[0:R, 0], in1=rtmp[0:R, 1], op=ALU.subtract), reads=[rrtmp], writes=[rA])
            P.op("pool", lambda e: e.tensor_tensor(out=Av[:, :, 8:16], in0=rtmp[0:R, 2], in1=rtmp[0:R, 3], op=ALU.add), reads=[rrtmp], writes=[rA])
            P.op("pool", lambda e: e.tensor_copy(out=qkb[0:R, 0:512].rearrange("p (j s d) -> p s j d", s=2, d=64),
                                                 in_=A[0:R, 0:512].rearrange("p (s j d) -> p s j d", s=2, d=64)), reads=[rA], writes=[rqkb])
            P.op("pool", lambda e: e.tensor_copy(out=qkb[0:R, 512:640], in_=A[0:R, 512:640]), reads=[rA], writes=[rqkb])
            if va_t is not None:
                P.op("pool", lambda e: e.tensor_copy(out=va_t[0:R, :, 0:64], in_=A[0:R, 640:768].rearrange("p (k d) -> p k d", d=64)), reads=[rA], writes=[rva_t])
            if last:
                P.op("sp", lambda e: e.dma_start(out=kp_d, in_=A[:, 512:640]), reads=[rA], dma=rA)
                P.op("sp", lambda e: e.dma_start(out=vp_d, in_=A[:, 640:768]), reads=[rA], dma=rA)
            yield

        def attn_transposes(R, akT_t, rakT, bank=None):
            bt, bb, rb = bank if bank is not None else gbank()
            for j in range(4):
                P.op("pe", lambda e, j=j: e.transpose(out=bb[:, j * 128:j * 128 + R], in_=qkb[0:R, j * 128:(j + 1) * 128], identity=ident[0:R, 0:R]),
                     reads=[rqkb, rcbf], writes=[rb], signal=False)
            P.op("pe", lambda e: e.transpose(out=bb[:, 512:512 + R], in_=qkb[0:R, 512:640], identity=ident[0:R, 0:R]), reads=[rqkb, rcbf], writes=[rb])
            P.op("act", lambda e: e.activation(out=aqT[:, :, 0:R], in_=bb[:, 0:512].rearrange("p (j t) -> p j t", t=128)[:, :, 0:R], func=AF.Copy),
                 reads=[rb], writes=[rb, raqT])
            P.op("act", lambda e: e.activation(out=akT_t[:, 0:R], in_=bb[:, 512:512 + R], func=AF.Copy), reads=[rb], writes=[rb, rakT])

        def mlstm_transposes(R, bank=None):
            bt, bb, rb = bank if bank is not None else gbank()
            for j in range(8):
                P.op("pe", lambda e, j=j: e.transpose(out=bb[:, j * 128:j * 128 + R], in_=qkv[0:R, j // 4, (j % 4) * 128:(j % 4 + 1) * 128],
                                                      identity=ident[0:R, 0:R]), reads=[rqkv, rcbf], writes=[rb], signal=(j == 7))
            P.op("act", lambda e: e.activation(out=TR[:, :, 0:R], in_=bb[:, :].rearrange("p (j t) -> p j t", t=128)[:, :, 0:R], func=AF.Copy),
                 reads=[rb], writes=[rb, rTR])

        def mlstm_finish(R, Nb_t, Db_ap, rbN, rbD, thr_ap):
            P.op("act", lambda e: e.activation(out=sm[0:R, 8:12], in_=Db_ap, func=AF.Square, scale=EPS ** 0.5), reads=[rbD], writes=[rbD, rsm])
            P.op("act", lambda e: e.activation(out=htmp[0:R, :], in_=Nb_t[0:R, 0:512], func=AF.Square), reads=[rbN], writes=[rbN, rhtmp])
            P.op("dve", lambda e: e.tensor_tensor(out=sm[0:R, 12:16], in0=sm[0:R, 8:12], in1=thr_ap, op=ALU.max), reads=[rsm, rX12], writes=[rsm])
            P.op("dve", lambda e: e.tensor_reduce(out=sm[0:R, 20:24], in_=htmp[0:R, :].rearrange("p (h d) -> p h d", d=128), axis=AX.X, op=ALU.add),
                 reads=[rhtmp], writes=[rhtmp, rsm])
            P.op("dve", lambda e: e.scalar_tensor_tensor(out=sm[0:R, 24:28], in0=sm[0:R, 20:24], scalar=1.0 / 128, in1=sm[0:R, 12:16], op0=ALU.mult, op1=ALU.add),
                 reads=[rsm], writes=[rsm])
            P.op("pool", lambda e: e.tensor_tensor(out=sm[0:R, 32:36], in0=sm[0:R, 24:28], in1=nhalf[0:R, 0:4], op=ALU.pow), reads=[rsm, rnhalf], writes=[rsm])
            for h in range(4):
                P.op("dve", lambda e, h=h: e.scalar_tensor_tensor(out=ym[0:R, h * 128:(h + 1) * 128], in0=Nb_t[0:R, h * 128:(h + 1) * 128],
                                                                  scalar=sm[0:R, 32 + h:33 + h], in1=Gm[0:R, h * 128:(h + 1) * 128], op0=ALU.mult, op1=ALU.mult),
                     reads=[rbN, rsm, rGm], writes=[rbN, rym])

        def attn_finish(R, banks):
            for kv in range(2):
                bt, rb = banks[kv]
                ov = bt[0:R, 0:260].rearrange("p (g d) -> p g d", d=65)
                P.op("dve", lambda e, ov=ov, kv=kv: e.tensor_tensor(out=smA[0:R, 40 + kv * 4:44 + kv * 4], in0=ov[:, :, 64], in1=esk[0:R, kv * 4:kv * 4 + 4], op=ALU.add),
                     reads=[rb, resk], writes=[rb, rsmA])
            P.op("dve", lambda e: e.reciprocal(out=smA[0:R, 48:56], in_=smA[0:R, 40:48]), reads=[rsmA], writes=[rsmA])
            P.op("dve", lambda e: e.tensor_scalar(out=smA[0:R, 48:56], in0=smA[0:R, 48:56], scalar1=0.5, scalar2=None, op0=ALU.mult), reads=[rsmA], writes=[rsmA])
            P.op("dve", lambda e: e.tensor_tensor(out=Ga[0:R, :].rearrange("p (h d) -> p h d", d=64), in0=Ga[0:R, :].rearrange("p (h d) -> p h d", d=64),
                                                   in1=smA[0:R, 48:56, None].broadcast_to([R, 8, 64]), op=ALU.mult), reads=[rGa, rsmA], writes=[rGa])
            for kv in range(2):
                bt, rb = banks[kv]
                ov = bt[0:R, 0:260].rearrange("p (g d) -> p g d", d=65)
                P.op("dve", lambda e, ov=ov, kv=kv: e.tensor_tensor(out=ym[0:R, 512 + kv * 256:512 + (kv + 1) * 256].rearrange("p (g d) -> p g d", d=64),
                                                                    in0=ov[:, :, 0:64], in1=Ga[0:R, kv * 256:(kv + 1) * 256].rearrange("p (g d) -> p g d", d=64),
                                                                    op=ALU.mult), reads=[rb, rGa], writes=[rb, rym])

        def back(R, xt, rxt, gate_t, rgate_t, y_dram, gated=False):
            bt, bb, rb = gbank()
            for j in range(8):
                P.op("pe", lambda e, j=j: e.transpose(out=bb[:, j * 128:j * 128 + R], in_=ym[0:R, j * 128:(j + 1) * 128], identity=ident[0:R, 0:R]),
                     reads=[rym, rcbf], writes=[rb], signal=(j == 7))
            P.op("act", lambda e: e.activation(out=TR[:, :, 0:R], in_=bb[:, :].rearrange("p (j t) -> p j t", t=128)[:, :, 0:R], func=AF.Copy),
                 reads=[rb], writes=[rb, rTR])
            yield
            for half in range(2):
                cs = slice(half * 512, (half + 1) * 512)
                btm, _, rbm = gbank()
                for kc in range(4):
                    P.op("pe", lambda e, kc=kc, btm=btm: e.matmul(btm[0:R, :], lhsT=TR[:, kc, 0:R], rhs=wmo[:, kc, cs], start=(kc == 0), stop=(kc == 3)),
                         reads=[rTR, rwmo], writes=[rbm], signal=(kc == 3))
                bta, _, rba = gbank()
                for kc in range(4):
                    P.op("pe", lambda e, kc=kc, bta=bta: e.matmul(bta[0:R, :], lhsT=TR[:, 4 + kc, 0:R], rhs=wao[:, kc, cs], start=(kc == 0), stop=(kc == 3)),
                         reads=[rTR, rwao], writes=[rba], signal=(kc == 3))
                P.op("dve", lambda e, btm=btm, half=half: e.scalar_tensor_tensor(out=u[0:R, half * 512:(half + 1) * 512], in0=thg[0:R, half, :], scalar=1.0, in1=btm[0:R, :],
                                                                                 op0=ALU.add, op1=ALU.mult), reads=[rthg, rbm], writes=[rbm, ru])
                P.op("dve", lambda e, bta=bta, half=half: e.scalar_tensor_tensor(out=t12[0:R, :], in0=thg[0:R, 2 + half, :], scalar=1.0, in1=bta[0:R, :],
                                                                                 op0=ALU.add, op1=ALU.mult), reads=[rthg, rba], writes=[rba, rt12])
                P.op("dve", lambda e, cs=cs: e.tensor_tensor(out=u[0:R, cs], in0=u[0:R, cs], in1=t12[0:R, :], op=ALU.add), reads=[rt12, ru], writes=[ru])
                yield
            bt, bb, rb = gbank()
            for j in range(8):
                P.op("pe", lambda e, j=j: e.transpose(out=bb[:, j * 128:j * 128 + R], in_=u[0:R, j * 128:(j + 1) * 128], identity=ident[0:R, 0:R]),
                     reads=[ru, rcbf], writes=[rb], signal=(j == 7))
            P.op("act", lambda e: e.activation(out=TR[:, :, 0:R], in_=bb[:, :].rearrange("p (j t) -> p j t", t=128)[:, :, 0:R], func=AF.Copy),
                 reads=[rb], writes=[rb, rTR])
            yield
            for half in range(2):
                cs = slice(half * 512, (half + 1) * 512)
                btf, _, rbf = gbank()
                for kc in range(8):
                    P.op("pe", lambda e, kc=kc, btf=btf: e.matmul(btf[0:R, :], lhsT=TR[:, kc, 0:R], rhs=wo[:, kc, cs], start=(kc == 0), stop=(kc == 7)),
                         reads=[rTR, rwo], writes=[rbf], signal=(kc == 7))
                if gated:
                    P.op("dve", lambda e, btf=btf, cs=cs: e.tensor_tensor(out=xt[0:R, cs], in0=btf[0:R, :], in1=xt[0:R, cs], op=ALU.add),
                         reads=[rbf, rxt], writes=[rbf, rxt])
                else:
                    P.op("dve", lambda e, btf=btf, cs=cs: e.tensor_tensor(out=t3[0:R, :], in0=btf[0:R, :], in1=gate_t[0:R, cs], op=ALU.mult),
                         reads=[rbf, rgate_t], writes=[rbf, rt3])
                    P.op("dve", lambda e, cs=cs: e.tensor_tensor(out=xt[0:R, cs], in0=xt[0:R, cs], in1=t3[0:R, :], op=ALU.add), reads=[rt3, rxt], writes=[rxt])
            yield
            act_accum(htmp[0:R, :], xt[0:R, 0:512], smB[0:R, 60:61], [rxt], [rhtmp, rsmB])
            P.op("dve", lambda e: e.tensor_scalar(out=smB[0:R, 62:63], in0=smB[0:R, 60:61], scalar1=1.0 / D, scalar2=EPS, op0=ALU.mult, op1=ALU.add),
                 reads=[rsmB], writes=[rsmB])
            act_accum(htmp[0:R, :], xt[0:R, 512:1024], smB[0:R, 61:62], [rxt], [rhtmp, rsmB])
            P.op("dve", lambda e: e.scalar_tensor_tensor(out=smB[0:R, 62:63], in0=smB[0:R, 61:62], scalar=1.0 / D, in1=smB[0:R, 62:63], op0=ALU.mult, op1=ALU.add),
                 reads=[rsmB], writes=[rsmB])
            P.op("pool", lambda e: e.tensor_tensor(out=smB[0:R, 63:64], in0=smB[0:R, 62:63], in1=nhalf[0:R, 0:1], op=ALU.pow), reads=[rsmB, rnhalf], writes=[rsmB])
            P.op("dve", lambda e: e.scalar_tensor_tensor(out=xt[0:R, :], in0=xt[0:R, :], scalar=smB[0:R, 63:64], in1=gfin[0:R, :], op0=ALU.mult, op1=ALU.mult),
                 reads=[rxt, rsmB, rgfin], writes=[rxt])
            P.op("sp", lambda e: e.dma_start(out=y_dram, in_=xt[0:R, :]), reads=[rxt], writes=[], dma=rxt)
            yield

        def gate_common(R):
            P.op("dve", lambda e: e.tensor_tensor(out=sm[0:R, 68:72], in0=G8[0:R, 0:4], in1=bif[0:R, 0:4], op=ALU.add), reads=[rG8, rbif], writes=[rsm])
            P.op("dve", lambda e: e.tensor_tensor(out=sm[0:R, 64:68], in0=G8[0:R, 4:8], in1=bif[0:R, 4:8], op=ALU.add), reads=[rG8, rbif], writes=[rsm])
            P.op("act", lambda e: e.activation(out=sm[0:R, 64:68], in_=sm[0:R, 64:68], func=AF.Exp, scale=-1.0), reads=[rsm], writes=[rsm])
            P.op("act", lambda e: e.activation(out=sm[0:R, 64:68], in_=sm[0:R, 64:68], func=AF.Ln, bias=1.0), reads=[rsm], writes=[rsm])
            P.op("dve", lambda e: e.tensor_scalar(out=sm[0:R, 64:68], in0=sm[0:R, 64:68], scalar1=-1.0, scalar2=None, op0=ALU.mult), reads=[rsm], writes=[rsm])


        def sample_tile(ss):
            R = 64
            ncs = [0]

            def TS(shape, dt, name):
                ncs[0] += 1
                nm = "s%s_%d" % (name, ncs[0])
                return ss.enter_context(nc.sbuf_tensor(nm, list(shape), dt)), Res(nm)
            xt, rxt = xs2[0]
            for _ in front(R, xs_d, xt, rxt, True, cst[0:64, C_RS:C_RS + 16], vaug[0][0], vaug[0][1]):
                pass
            for _ in zgates(R):
                pass
            stm_t, rstm = TS([16, 4], F32, "stm")
            msn, rmsn = TS([64, 4], F32, "msn")
            Wd, rWd = TS([64, 16, 4], F32, "Wd")
            wCb, rwCb = TS([128, 16, 4], F32, "wCb")
            nS, rnS = TS([128, 16, 4], F32, "nS")
            nld, rnld = TS([64, 128], F32, "nld")
            Ch = [TS([128, 128], F32, "Ch%d" % i) for i in range(NRING)]
            Cb = [TS([128, 128], BF16, "Cb%d" % i) for i in range(NRING)]
            nb1, rnb1 = TS([128, 16, 4], BF16, "nb1")
            qm = [TS([128, 64], BF16, "qm%d" % i) for i in range(4)]
            kwb = [TS([64, 128], BF16, "kwb%d" % i) for i in range(4)]
            ckr = [TS([128, 128], BF16, "ck%d" % i) for i in range(6)]
            ckTr = [TS([128, 128], BF16, "ckT%d" % i) for i in range(2)]
            cvr = [TS([128, 2, 66], BF16, "cv%d" % i) for i in range(6)]
            PF = [TS([128, 8, 64], BF16, "PF%d" % i) for i in range(2)]
            pcx = [TS([128, 32], BF16, "pcx%d" % i) for i in range(2)]
            P.op("sp", lambda e: e.dma_start(out=stm_t[:], in_=stm_d), writes=[rstm], dma=rstm)
            P.op("sp", lambda e: e.dma_start(out=nld[:], in_=stn_d), writes=[rnld], dma=rnld)
            for (t_, r_) in qm + PF:
                P.op("pool", lambda e, t_=t_: e.memset(t_[:], 0.0), writes=[r_])
            for (t_, r_) in cvr:
                P.op("pool", lambda e, t_=t_: e.memset(t_[:], 1.0), writes=[r_])
            gate_common(R)
            bt, bb, rb = gbank()
            P.op("pe", lambda e: e.matmul(bt[0:64, 0:4], lhsT=cst[0:64, C_BC:C_BC + 64], rhs=sm[0:64, 64:68], start=True, stop=True), reads=[rcst, rsm], writes=[rb])
            P.op("pe", lambda e: e.matmul(bt[0:64, 4:8], lhsT=cst[0:64, C_BS:C_BS + 64], rhs=sm[0:64, 64:68], start=True, stop=True), reads=[rcst, rsm], writes=[rb])
            P.op("dve", lambda e: e.tensor_copy(out=sm[0:64, 72:80], in_=bt[0:64, 0:8]), reads=[rb], writes=[rb, rsm])
            P.op("dve", lambda e: e.tensor_tensor(out=sm[0:64, 80:84], in0=sm[0:64, 68:72], in1=sm[0:64, 72:76], op=ALU.subtract), reads=[rsm], writes=[rsm])
            bt, bb, rb = gbank()
            for k, off in enumerate((C_P1, C_P2, C_P3)):
                P.op("pe", lambda e, k=k, off=off: e.matmul(bt[0:64, k * 4:k * 4 + 4], lhsT=cst[0:64, off:off + 64], rhs=sm[0:64, 80:84], start=True, stop=True),
                     reads=[rcst, rsm], writes=[rb])
            P.op("pe", lambda e: e.matmul(bt[0:64, 12:16], lhsT=cst[0:16, C_BEXP:C_BEXP + 64], rhs=stm_t[:], start=True, stop=True), reads=[rcst, rstm], writes=[rb])
            P.op("dve", lambda e: e.tensor_copy(out=sm[0:64, 84:100 - 4], in_=bt[0:64, 0:12]), reads=[rb], writes=[rb, rsm])
            P.op("dve", lambda e: e.tensor_copy(out=E12[0:64, 0:4], in_=bt[0:64, 12:16]), reads=[rb], writes=[rb, rE12])
            P.op("dve", lambda e: e.tensor_tensor(out=sm[0:64, 84:88], in0=sm[0:64, 84:88], in1=sm[0:64, 88:92], op=ALU.max), reads=[rsm], writes=[rsm])
            P.op("dve", lambda e: e.tensor_tensor(out=sm[0:64, 84:88], in0=sm[0:64, 84:88], in1=sm[0:64, 92:96], op=ALU.max), reads=[rsm], writes=[rsm])
            P.op("dve", lambda e: e.tensor_tensor(out=sm[0:64, 84:88], in0=sm[0:64, 84:88], in1=sm[0:64, 80:84], op=ALU.max), reads=[rsm], writes=[rsm])
            P.op("dve", lambda e: e.tensor_tensor(out=sm[0:64, 92:96], in0=sm[0:64, 84:88], in1=E12[0:64, 0:4], op=ALU.max), reads=[rsm, rE12], writes=[rsm])
            P.op("dve", lambda e: e.tensor_tensor(out=E12[0:64, 0:4], in0=E12[0:64, 0:4], in1=sm[0:64, 92:96], op=ALU.subtract), reads=[rsm, rE12], writes=[rE12])
            P.op("dve", lambda e: e.tensor_tensor(out=E12[0:64, 4:8], in0=sm[0:64, 80:84], in1=sm[0:64, 92:96], op=ALU.subtract), reads=[rsm], writes=[rE12])
            P.op("dve", lambda e: e.scalar_tensor_tensor(out=E12[0:64, 8:12], in0=sm[0:64, 72:76], scalar=-1.0, in1=sm[0:64, 92:96], op0=ALU.mult, op1=ALU.subtract),
                 reads=[rsm], writes=[rE12])
            P.op("dve", lambda e: e.tensor_scalar(out=E12[0:64, 12:16], in0=E12[0:64, 8:12], scalar1=2.0, scalar2=LN_EPS, op0=ALU.mult, op1=ALU.add),
                 reads=[rE12], writes=[rE12])
            P.op("dve", lambda e: e.tensor_tensor(out=msn[:], in0=sm[0:64, 76:80], in1=sm[0:64, 92:96], op=ALU.add), reads=[rsm], writes=[rmsn])
            P.op("act", lambda e: e.activation(out=X12[0:64, :], in_=E12[0:64, :], func=AF.Exp), reads=[rE12], writes=[rX12])
            for b in range(16):
                P.op("sp", lambda e, b=b: e.dma_start(out=ms_d[b:b + 1, :], in_=msn[4 * b:4 * b + 1, :]), reads=[rmsn], dma=rmsn)
            wC = X12[0:64, 0:4]; w = X12[0:64, 4:8]; thr = X12[0:64, 12:16]
            P.op("dve", lambda e: e.tensor_tensor(out=Wd[:], in0=wC[:, None, :].broadcast_to([64, 16, 4]),
                                                  in1=cst[0:64, C_BM0:C_BM0 + 16, None].broadcast_to([64, 16, 4]), op=ALU.mult), reads=[rX12, rcst], writes=[rWd])
            bt, bb, rb = gbank()
            P.op("pe", lambda e: e.matmul(bt[:, 0:64], lhsT=onesf[0:64, :], rhs=Wd[:].rearrange("p b h -> p (b h)"), start=True, stop=True), reads=[rcst, rWd], writes=[rb])
            P.op("dve", lambda e: e.tensor_copy(out=wCb[:].rearrange("p b h -> p (b h)"), in_=bt[:, 0:64]), reads=[rb], writes=[rb, rwCb])
            bt, bb, rb = gbank()
            P.op("pe", lambda e: e.transpose(out=bt[:, 0:64], in_=nld[:], identity=identf[0:64, 0:64]), reads=[rnld, rcst], writes=[rb])
            P.op("dve", lambda e: e.tensor_copy(out=nS[:].rearrange("p b h -> p (b h)"), in_=bt[:, 0:64]), reads=[rb], writes=[rb, rnS])
            P.op("dve", lambda e: e.tensor_tensor(out=nS[:], in0=nS[:], in1=wCb[:], op=ALU.mult), reads=[rnS, rwCb], writes=[rnS])
            P.op("dve", lambda e: e.tensor_copy(out=nb1[:], in_=nS[:]), reads=[rnS], writes=[rnb1])
            obanks = []

            def s_m():
                mlstm_transposes(R, pb[3])
                P.op("dve", lambda e: e.tensor_tensor(out=kw[0:64, :].rearrange("p (h d) -> p h d", d=128), in0=qkv[0:64, 1, :].rearrange("p (h d) -> p h d", d=128),
                                                      in1=w[:, :, None].broadcast_to([64, 4, 128]), op=ALU.mult), reads=[rqkv, rX12], writes=[rkw])
                P.op("dve", lambda e: e.tensor_tensor(out=wm[0:64, :, 0:64], in0=cbf[0:64, None, 512:576].broadcast_to([64, 4, 64]),
                                                      in1=w[:, :, None].broadcast_to([64, 4, 64]), op=ALU.mult), reads=[rcbf, rX12], writes=[rwm])
                bs_t, _, rbs = pb[3]
                for h in range(4):
                    P.op("pe", lambda e, h=h: e.matmul(bs_t[0:64, h * 64:(h + 1) * 64], lhsT=TR[:, 4 + h, 0:64], rhs=TR[:, h, 0:64], start=True, stop=True),
                         reads=[rTR], writes=[rbs])
                P.op("dve", lambda e: e.tensor_tensor(out=Sp[0:64, :, 0:64], in0=bs_t[0:64, 0:256].rearrange("p (h t) -> p h t", t=64), in1=wm[0:64, :, 0:64], op=ALU.mult),
                     reads=[rbs, rwm], writes=[rbs, rSp])
                bn_t, _, rbn = pb[4]
                bd_t, _, rbd = pb[7]
                bnu_t, rbnu = pb[7][0][:, 8:72], pb[7][2]
                P.op("dve", lambda e: e.memset(bnu_t, 0.0), writes=[rbnu])
                it = 0

                def cload(k):
                    if k < 64:
                        h_, b_ = k // 16, k % 16
                        c_t_, rc__ = Ch[k % NRING]
                        P.op("act", lambda e: e.dma_start(out=c_t_[:], in_=stC_d[b_, h_]), writes=[rc__], dma=rc__)
                for k_ in range(PFD):
                    cload(k_)
                for h in range(4):
                    vh = qkv[0:64, 2, h * 128:(h + 1) * 128]
                    P.op("pe", lambda e, h=h, vh=vh: e.matmul(bn_t[0:64, h * 128:(h + 1) * 128], lhsT=Sp[0:64, h, 0:64], rhs=vh, start=True, stop=False, skip_group_check=True),
                         reads=[rSp, rqkv], writes=[rbn])
                    P.op("pe", lambda e, h=h: e.matmul(bd_t[0:64, h:h + 1], lhsT=Sp[0:64, h, 0:64], rhs=onesb[0:64, 0:1], start=True, stop=False, skip_group_check=True),
                         reads=[rSp, rcbf], writes=[rbd])
                    for b in range(16):
                        c_t, rc_ = Ch[it % NRING]; cb_t, rcb = Cb[it % NRING]; q_t, rq = qm[it % 4]; kb_t, rkb = kwb[it % 4]
                        cload(it + PFD)
                        it += 1
                        P.op("act", lambda e, c_t=c_t, cb_t=cb_t, b=b, h=h: e.activation(out=cb_t[:], in_=c_t[:], func=AF.Copy, scale=wCb[:, b, h:h + 1]),
                             reads=[rc_, rwCb], writes=[rcb])
                        P.op("pool", lambda e, q_t=q_t, b=b, h=h: e.tensor_copy(out=q_t[:, 4 * b:4 * b + 4], in_=TR[:, h, 4 * b:4 * b + 4]), reads=[rTR], writes=[rq])
                        P.op("pe", lambda e, q_t=q_t, cb_t=cb_t, h=h, b=b: e.matmul(bn_t[0:64, h * 128:(h + 1) * 128], lhsT=q_t[:, :], rhs=cb_t[:, :],
                                                                                   start=False, stop=(b == 15), skip_group_check=True), reads=[rq, rcb], writes=[rbn])
                        P.op("pe", lambda e, q_t=q_t, h=h, b=b: e.matmul(bd_t[0:64, h:h + 1], lhsT=q_t[:, :], rhs=nb1[:, b, h:h + 1],
                                                                         start=False, stop=(b == 15), skip_group_check=True), reads=[rq, rnb1], writes=[rbd])
                        P.op("pool", lambda e, q_t=q_t, b=b: e.memset(q_t[:, 4 * b:4 * b + 4], 0.0), writes=[rq])
                        P.op("dve", lambda e, kb_t=kb_t, b=b, h=h: e.tensor_scalar(out=kb_t[:], in0=kw[0:64, h * 128:(h + 1) * 128], scalar1=cst[0:64, C_BM + b:C_BM + b + 1],
                                                                                   scalar2=None, op0=ALU.mult), reads=[rkw, rcst], writes=[rkb])
                        bcu, _, rbcu = tbank()
                        P.op("pe", lambda e, kb_t=kb_t, bcu=bcu, vh=vh: e.matmul(bcu[:, 0:128], lhsT=kb_t[:, :], rhs=vh, start=True, stop=True), reads=[rkb, rqkv], writes=[rbcu])
                        P.op("pe", lambda e, kb_t=kb_t, b=b, h=h: e.matmul(bnu_t[:, b * 4 + h:b * 4 + h + 1], lhsT=kb_t[:, :], rhs=onesb[0:64, 0:1], start=False, stop=True, skip_group_check=True),
                             reads=[rkb, rcbf], writes=[rbnu])
                        P.op("dve", lambda e, c_t=c_t, bcu=bcu, b=b, h=h: e.scalar_tensor_tensor(out=c_t[:], in0=c_t[:], scalar=wCb[:, b, h:h + 1], in1=bcu[:, 0:128],
                                                                                               op0=ALU.mult, op1=ALU.add), reads=[rc_, rwCb, rbcu, rcb], writes=[rc_, rbcu])
                        P.op("sp", lambda e, c_t=c_t, b=b, h=h: e.dma_start(out=Cs_d[b, h], in_=c_t[:]), reads=[rc_], dma=rc_)
                        yield
                P.op("dve", lambda e: e.tensor_tensor(out=nS[:].rearrange("p b h -> p (b h)"), in0=nS[:].rearrange("p b h -> p (b h)"), in1=bnu_t[:, 0:64], op=ALU.add),
                     reads=[rnS, rbnu, rnb1], writes=[rnS, rbnu])
                bt, bb, rb = tbank()
                P.op("pe", lambda e: e.transpose(out=bt[0:64, 0:128], in_=nS[:].rearrange("p b h -> p (b h)"), identity=identf), reads=[rnS, rcst], writes=[rb])
                P.op("dve", lambda e: e.tensor_copy(out=nld[:], in_=bt[0:64, 0:128]), reads=[rb], writes=[rb, rnld])
                P.op("sp", lambda e: e.dma_start(out=ns_d, in_=nld[:]), reads=[rnld], dma=rnld)
                mlstm_finish(R, bn_t, bd_t[0:64, 0:4], rbn, rbd, thr)
                yield

            def s_a():
                ak_c, rak_c = akT[0]
                va_c, rva_c = vaug[0]
                attn_transposes(R, ak_c, rak_c, pb[5])
                yield
                rko = Res("ko")
                P.op("sp", lambda e: e.dma_start(out=ks_d[:, 0:124, :], in_=ck_d[:, 4:128, :]), dma=rko)
                P.op("sp", lambda e: e.dma_start(out=vs_d[:, 0:124, :], in_=cv_d[:, 4:128, :]), dma=rko)
                for b in range(16):
                    P.op("sp", lambda e, b=b: e.dma_start(out=ks_d[b, 124:128, :], in_=A[4 * b:4 * b + 4, 512:640]), reads=[rA], dma=rA)
                    P.op("sp", lambda e, b=b: e.dma_start(out=vs_d[b, 124:128, :], in_=A[4 * b:4 * b + 4, 640:768]), reads=[rA], dma=rA)
                for kv in range(2):
                    ps = slice(kv * 64, (kv + 1) * 64)
                    bsn, _, rbsn = pb[5 + kv]
                    for j in range(4):
                        P.op("pe", lambda e, kv=kv, j=j, ps=ps, bsn=bsn: e.matmul(bsn[0:64, j * 64:(j + 1) * 64], lhsT=ak_c[ps, 0:64], rhs=aqT[ps, j, 0:64],
                                                                                  start=True, stop=True), reads=[rak_c, raqT], writes=[rbsn])
                    P.op("act", lambda e, kv=kv, bsn=bsn: e.activation(out=Pp[0:64, kv * 256:(kv + 1) * 256], in_=bsn[0:64, 0:256], func=AF.Exp, scale=0.125),
                         reads=[rbsn], writes=[rbsn, rPp])
                P.op("pool", lambda e: e.tensor_tensor(out=Pp[0:64, :].rearrange("p (g t) -> p g t", t=64), in0=Pp[0:64, :].rearrange("p (g t) -> p g t", t=64),
                                                       in1=cbf[0:64, None, 512:576].broadcast_to([64, 8, 64]), op=ALU.mult), reads=[rPp, rcbf], writes=[rPp])
                yield
                obanks.extend([(pb[5][0], pb[5][2]), (pb[6][0], pb[6][2])])
                for kv in range(2):
                    bo, rbo = obanks[kv]
                    P.op("dve", lambda e, bo=bo: e.memset(bo[0:64, 0:260], 0.0), writes=[rbo])
                    for g in range(4):
                        P.op("pe", lambda e, bo=bo, g=g, kv=kv: e.matmul(bo[0:64, g * 65:(g + 1) * 65], lhsT=Pp[0:64, (kv * 4 + g) * 64:(kv * 4 + g + 1) * 64], rhs=va_c[0:64, kv, :],
                                                                         start=False, stop=False, skip_group_check=True), reads=[rPp, rva_c], writes=[rbo])
                def kload(k):
                    if k < 16:
                        ck_t_, rck_ = ckr[k % 6]
                        cv_t_, rcv_ = cvr[k % 6]
                        P.op("pool", lambda e: e.dma_start(out=ck_t_[:], in_=ck_d[k]), writes=[rck_], dma=rck_)
                        for kv_ in range(2):
                            P.op("pool", lambda e, kv_=kv_: e.dma_start(out=cv_t_[:, kv_, 0:64], in_=cv_d[k][:, kv_ * 64:(kv_ + 1) * 64]), writes=[rcv_], dma=rcv_)
                for k_ in range(PFD):
                    kload(k_)
                for b in range(16):
                    ck_t, rck = ckr[b % 6]; ckT_t, rckT = ckTr[b % 2]; cv_t, rcv = cvr[b % 6]; pf_t, rpf = PF[b % 2]; px_t, rpx = pcx[b % 2]
                    kload(b + PFD)
                    bt, bb, rb = tbank()
                    P.op("pe", lambda e, ck_t=ck_t, bb=bb: e.transpose(out=bb[:, 0:128], in_=ck_t[:], identity=ident), reads=[rck, rcbf], writes=[rb])
                    P.op("act", lambda e, ckT_t=ckT_t, bb=bb: e.activation(out=ckT_t[:], in_=bb[:, 0:128], func=AF.Copy), reads=[rb], writes=[rb, rckT])
                    for kv in range(2):
                        ps = slice(kv * 64, (kv + 1) * 64)
                        bsc, _, rbsc = tbank()
                        for j in range(4):
                            P.op("pe", lambda e, kv=kv, j=j, ps=ps, b=b, ckT_t=ckT_t, bsc=bsc: e.matmul(bsc[:, j * 4:(j + 1) * 4], lhsT=ckT_t[ps, :],
                                                                                                     rhs=aqT[ps, j, 4 * b:4 * b + 4], start=True, stop=True),
                                 reads=[rckT, raqT], writes=[rbsc])
                        P.op("act", lambda e, px_t=px_t, bsc=bsc, kv=kv: e.activation(out=px_t[:, kv * 16:(kv + 1) * 16], in_=bsc[:, 0:16], func=AF.Exp, scale=0.125),
                             reads=[rbsc], writes=[rbsc, rpx])
                    P.op("dve", lambda e, px_t=px_t, pf_t=pf_t, b=b: e.tensor_tensor(out=pf_t[:, :, 4 * b:4 * b + 4], in0=px_t[:].rearrange("p (h t) -> p h t", t=4),
                                                                                     in1=cbf[:, None, 256:260].broadcast_to([128, 8, 4]), op=ALU.mult),
                         reads=[rpx, rcbf], writes=[rpf])
                    for kv in range(2):
                        bo, rbo = obanks[kv]
                        for g in range(4):
                            P.op("pe", lambda e, bo=bo, g=g, kv=kv, pf_t=pf_t, cv_t=cv_t: e.matmul(bo[0:64, g * 65:(g + 1) * 65], lhsT=pf_t[:, kv * 4 + g, :], rhs=cv_t[:, kv, 0:65],
                                                                                                 start=False, stop=(b == 15), skip_group_check=True), reads=[rpf, rcv], writes=[rbo])
                    P.op("pool", lambda e, pf_t=pf_t, b=b: e.memset(pf_t[:, :, 4 * b:4 * b + 4], 0.0), writes=[rpf])
                    yield
                yield

            chains = [s_m(), s_a()]
            while chains:
                for g in list(chains):
                    try:
                        next(g)
                    except StopIteration:
                        chains.remove(g)
            attn_finish(R, obanks)
            for _ in back(R, xt, rxt, gatehs, rgatehs, ys_d):
                pass

        if RUN_SAMPLE:
            with contextlib.ExitStack() as ss:
                sample_tile(ss)
                P.barrier()

        for k_ in range(8):
            P.op("pool", lambda e, k_=k_: e.tensor_tensor(out=wo[:, k_, :], in0=wo[:, k_, :], in1=gateh[:, :], op=ALU.mult), reads=[rwo, rgateh], writes=[rwo])
        P.op("pool", lambda e: e.memset(Cst[:], 0.0), writes=[rCst])
        P.op("pool", lambda e: e.memset(nst[:], 0.0), writes=[rnst])
        P.op("pool", lambda e: e.memset(mprev[0][0][:], 0.0), writes=[mprev[0][1]])

        def prompt_front(i):
            xt, rxt = xs2[i % 2]
            va_t, rva_t = vaug[i % 3]
            return front(128, xp_d[i * 128:(i + 1) * 128, :], xt, rxt, False, cst[:, C_RP + i * 16:C_RP + (i + 1) * 16],
                         va_t, rva_t, last=(i == NT_RUN - 1))

        def mix(i):
            R = 128
            xt, rxt = xs2[i % 2]
            yield from zgates(R)
            chains = [mchain(i), achain(i)]
            while chains:
                for g in list(chains):
                    try:
                        next(g)
                    except StopIteration:
                        chains.remove(g)
                    yield
            if i == 0:
                dump("X12", X12[:], rX12); dump("ym", ym[:], rym); dump("sm", sm[:], rsm); dump("Cst", Cst[:], rCst)
            yield from back(R, xt, rxt, gateh, rgateh, yp_d[i * 128:(i + 1) * 128, :], gated=True)

        def mchain(i):
            R = 128
            mp_t, rmp = mprev[i % 2]
            mn_t, rmn = mprev[(i + 1) % 2]
            gate_common(R)
            bS, _, rbS = pb[7]
            bt, rb = bS, rbS
            P.op("pe", lambda e: e.matmul(bt[:, 8:12], lhsT=cst[:, C_LE:C_LE + 128], rhs=sm[:, 64:68], start=True, stop=True), reads=[rcst, rsm], writes=[rb])
            P.op("pe", lambda e: e.matmul(bt[:, 12:16], lhsT=onesf, rhs=sm[:, 64:68], start=True, stop=True), reads=[rcst, rsm], writes=[rb])
            P.op("dve", lambda e: e.tensor_copy(out=sm[:, 72:80], in_=bt[:, 8:16]), reads=[rb], writes=[rb, rsm])
            P.op("dve", lambda e: e.tensor_tensor(out=sm[:, 80:84], in0=sm[:, 68:72], in1=sm[:, 72:76], op=ALU.subtract), reads=[rsm], writes=[rsm])
            P.op("pe", lambda e: e.transpose(out=bt[0:4, 16:144], in_=sm[:, 80:84], identity=identf), reads=[rsm, rcst], writes=[rb])
            P.op("dve", lambda e: e.tensor_reduce(out=sm[0:4, 84:85], in_=bt[0:4, 16:144], axis=AX.X, op=ALU.max), reads=[rb], writes=[rb, rsm])
            P.op("dve", lambda e: e.tensor_scalar(out=sm[0:4, 88:92], in0=identf[0:4, 0:4], scalar1=sm[0:4, 84:85], scalar2=None, op0=ALU.mult),
                 reads=[rsm, rcst], writes=[rsm])
            P.op("pe", lambda e: e.matmul(bt[:, 144:148], lhsT=onesf[0:4, :], rhs=sm[0:4, 88:92], start=True, stop=True), reads=[rcst, rsm], writes=[rb])
            P.op("dve", lambda e: e.tensor_tensor(out=sm[:, 92:96], in0=bt[:, 144:148], in1=mp_t[:], op=ALU.max), reads=[rb, rmp], writes=[rb, rsm])
            P.op("dve", lambda e: e.tensor_tensor(out=E12[:, 0:4], in0=mp_t[:], in1=sm[:, 92:96], op=ALU.subtract), reads=[rmp, rsm], writes=[rE12])
            P.op("dve", lambda e: e.tensor_tensor(out=E12[:, 4:8], in0=sm[:, 80:84], in1=sm[:, 92:96], op=ALU.subtract), reads=[rsm], writes=[rE12])
            P.op("dve", lambda e: e.scalar_tensor_tensor(out=E12[:, 8:12], in0=sm[:, 72:76], scalar=-1.0, in1=sm[:, 92:96], op0=ALU.mult, op1=ALU.subtract),
                 reads=[rsm], writes=[rE12])
            P.op("dve", lambda e: e.tensor_scalar(out=E12[:, 12:16], in0=E12[:, 8:12], scalar1=2.0, scalar2=LN_EPS, op0=ALU.mult, op1=ALU.add),
                 reads=[rE12], writes=[rE12])
            P.op("dve", lambda e: e.tensor_tensor(out=mn_t[:], in0=sm[:, 76:80], in1=sm[:, 92:96], op=ALU.add), reads=[rsm], writes=[rmn])
            P.op("act", lambda e: e.activation(out=X12[:], in_=E12[:], func=AF.Exp), reads=[rE12], writes=[rX12])
            wC = X12[:, 0:4]; w = X12[:, 4:8]; thr = X12[:, 12:16]
            yield
            mlstm_transposes(R, pb[3])
            yield
            P.op("dve", lambda e: e.tensor_tensor(out=kw[:].rearrange("p (h d) -> p h d", d=128), in0=qkv[:, 1, :].rearrange("p (h d) -> p h d", d=128),
                                                  in1=w[:, :, None].broadcast_to([128, 4, 128]), op=ALU.mult), reads=[rqkv, rX12], writes=[rkw])
            P.op("dve", lambda e: e.tensor_tensor(out=Csb[:], in0=Cst[:], in1=wC[:, :, None].broadcast_to([128, 4, 128]), op=ALU.mult),
                 reads=[rCst, rX12], writes=[rCsb])
            P.op("dve", lambda e: e.tensor_tensor(out=nsb[:], in0=nst[:], in1=wC, op=ALU.mult), reads=[rnst, rX12], writes=[rnsb])
            P.op("dve", lambda e: e.tensor_tensor(out=wm[:], in0=cbf[:, None, 128:256].broadcast_to([128, 4, 128]),
                                                  in1=w[:, :, None].broadcast_to([128, 4, 128]), op=ALU.mult), reads=[rcbf, rX12], writes=[rwm])
            yield
            bs_t, _, rbs = pb[3]
            for h in range(4):
                P.op("pe", lambda e, h=h: e.matmul(bs_t[:, h * 128:(h + 1) * 128], lhsT=TR[:, 4 + h, :], rhs=TR[:, h, :], start=True, stop=True),
                     reads=[rTR], writes=[rbs], signal=(h == 3))
            P.op("dve", lambda e: e.tensor_tensor(out=Sp[:], in0=bs_t[:, :].rearrange("p (h t) -> p h t", t=128), in1=wm[:], op=ALU.mult),
                 reads=[rbs, rwm], writes=[rbs, rSp])
            yield
            bn_t, _, rbn = pb[4]
            bd_t, _, rbd = pb[7]
            bc_t, _, rbc = pb[3]
            for h in range(4):
                P.op("pe", lambda e, h=h: e.matmul(bn_t[:, h * 128:(h + 1) * 128], lhsT=Sp[:, h, :], rhs=qkv[:, 2, h * 128:(h + 1) * 128], start=True, stop=False),
                     reads=[rSp, rqkv], writes=[rbn], signal=False)
                P.op("pe", lambda e, h=h: e.matmul(bn_t[:, h * 128:(h + 1) * 128], lhsT=TR[:, h, :], rhs=Csb[:, h, :], start=False, stop=True),
                     reads=[rTR, rCsb], writes=[rbn], signal=(h == 3))
            for h in range(4):
                P.op("pe", lambda e, h=h: e.matmul(bd_t[:, h:h + 1], lhsT=Sp[:, h, :], rhs=onesb[:, 0:1], start=True, stop=False), reads=[rSp, rcbf], writes=[rbd], signal=False)
                P.op("pe", lambda e, h=h: e.matmul(bd_t[:, h:h + 1], lhsT=TR[:, h, :], rhs=nsb[:, h:h + 1], start=False, stop=True), reads=[rTR, rnsb], writes=[rbd], signal=False)
            for h in range(4):
                P.op("pe", lambda e, h=h: e.matmul(bd_t[:, 4 + h:5 + h], lhsT=kw[:, h * 128:(h + 1) * 128], rhs=onesb[:, 0:1], start=True, stop=True),
                     reads=[rkw, rcbf], writes=[rbd], signal=(h == 3))
            for h in range(4):
                P.op("pe", lambda e, h=h: e.matmul(bc_t[:, h * 128:(h + 1) * 128], lhsT=kw[:, h * 128:(h + 1) * 128], rhs=qkv[:, 2, h * 128:(h + 1) * 128],
                                                   start=True, stop=True), reads=[rkw, rqkv], writes=[rbc], signal=(h == 3))
            yield
            P.op("dve", lambda e: e.tensor_tensor(out=nst[:], in0=nst[:], in1=wC, op=ALU.mult), reads=[rnst, rX12, rnsb], writes=[rnst])
            P.op("dve", lambda e: e.tensor_tensor(out=nst[:], in0=nst[:], in1=bd_t[:, 4:8], op=ALU.add), reads=[rnst, rbd], writes=[rnst, rbd])
            for h in range(4):
                P.op("dve", lambda e, h=h: e.scalar_tensor_tensor(out=Cst[:, h, :], in0=Cst[:, h, :], scalar=wC[:, h:h + 1], in1=bc_t[:, h * 128:(h + 1) * 128],
                                                                  op0=ALU.mult, op1=ALU.add), reads=[rCst, rX12, rbc, rCsb], writes=[rCst, rbc])
            yield
            mlstm_finish(R, bn_t, bd_t[:, 0:4], rbn, rbd, thr)
            yield

        def achain(i):
            R = 128
            ak_c, rak_c = akT[i % 2]
            ak_p, rak_p = akT[(i + 1) % 2]
            va_c, rva_c = vaug[i % 3]
            va_p, rva_p = vaug[(i - 1) % 3]
            attn_transposes(R, ak_c, rak_c, pb[6])
            yield
            obanks = []
            for kv in range(2):
                ps = slice(kv * 64, (kv + 1) * 64)
                bsc, _, rbsc = pb[5]
                P.op("pe", lambda e, bsc=bsc, ps=ps: e.matmul(bsc[:, :], lhsT=ak_c[ps, :], rhs=aqT[ps, :, :], start=True, stop=True), reads=[rak_c, raqT], writes=[rbsc])
                P.op("act", lambda e, bsc=bsc: e.activation(out=Pc[:], in_=bsc[:, :], func=AF.Exp, scale=0.125), reads=[rbsc], writes=[rbsc, rPc])
                P.op("dve", lambda e: e.tensor_tensor(out=Pc[:].rearrange("p (g t) -> p g t", t=128), in0=Pc[:].rearrange("p (g t) -> p g t", t=128),
                                                       in1=cbf[:, None, 128:256].broadcast_to([128, 4, 128]), op=ALU.mult), reads=[rPc, rcbf], writes=[rPc])
                if i > 0:
                    bsp, _, rbsp = pb[5]
                    P.op("pe", lambda e, bsp=bsp, ps=ps: e.matmul(bsp[:, :], lhsT=ak_p[ps, :], rhs=aqT[ps, :, :], start=True, stop=True), reads=[rak_p, raqT], writes=[rbsp])
                    P.op("act", lambda e, bsp=bsp: e.activation(out=Pp[:], in_=bsp[:, :], func=AF.Exp, scale=0.125), reads=[rbsp], writes=[rbsp, rPp])
                    P.op("dve", lambda e: e.tensor_tensor(out=Pp[:].rearrange("p (g t) -> p g t", t=128), in0=Pp[:].rearrange("p (g t) -> p g t", t=128),
                                                           in1=cbf[:, None, 256:384].broadcast_to([128, 4, 128]), op=ALU.mult), reads=[rPp, rcbf], writes=[rPp])
                yield
                bo, rbo = (pb[6][0][:, 0:260], pb[6][2]) if kv == 0 else (pb[7][0][:, 200:460], pb[7][2])
                for g in range(4):
                    if i > 0:
                        P.op("pe", lambda e, g=g, bo=bo, kv=kv: e.matmul(bo[:, g * 65:(g + 1) * 65], lhsT=Pp[:, g * 128:(g + 1) * 128], rhs=va_p[:, kv, :],
                                                                         start=True, stop=False), reads=[rPp, rva_p], writes=[rbo], signal=False)
                    P.op("pe", lambda e, g=g, bo=bo, kv=kv: e.matmul(bo[:, g * 65:(g + 1) * 65], lhsT=Pc[:, g * 128:(g + 1) * 128], rhs=va_c[:, kv, :],
                                                                     start=(i == 0), stop=True), reads=[rPc, rva_c], writes=[rbo], signal=(g == 3))
                obanks.append((bo, rbo))
                yield
            attn_finish(R, obanks)
            yield

        alloc_fset()

        def step(g, p):
            use_set(p)
            try:
                next(g)
                return True
            except StopIteration:
                return False

        g0 = prompt_front(0)
        while step(g0, 0):
            pass
        for i in range(NT_RUN):
            gm_ = mix(i)
            gf_ = prompt_front(i + 1) if i + 1 < NT_RUN else None
            am, af = True, gf_ is not None
            while am or af:
                for _r in range(MIXSTEPS):
                    if am:
                        am = step(gm_, i % 2)
                for _r in range(FSTEPS):
                    if af:
                        af = step(gf_, (i + 1) % 2)
        use_set(0)
        P.op("sp", lambda e: e.dma_start(out=Cp_d.rearrange("h d e -> d h e"), in_=Cst[:]), reads=[rCst], dma=rCst)
        bt, bb, rb = gbank()
        P.op("pe", lambda e: e.transpose(out=bt[0:4, 0:128], in_=nst[:], identity=identf), reads=[rnst, rcst], writes=[rb])
        nT, rnT = htmp[0:4, 0:128], rhtmp
        P.op("dve", lambda e: e.tensor_copy(out=nT, in_=bt[0:4, 0:128]), reads=[rb], writes=[rb, rnT])
        P.op("sp", lambda e: e.dma_start(out=np_d, in_=nT), reads=[rnT], dma=rnT)
        mfin, rmfin = mprev[NT_RUN % 2]
        P.op("sp", lambda e: e.dma_start(out=mp_d, in_=mfin[0:1, :]), reads=[rmfin], dma=rmfin)

        P.barrier()
    return nc


_CACHE = {}


def kernel(x_prompt, x_sample, state_C, state_n, state_m, cache_k, cache_v, c_prompt, c_sample,
           w_ada, b_ada, g_norm, w_in, b_igate, b_fgate, g_mnorm, sinks, w_m_out, w_a_out, w_out, g_final):
    f = lambda a: np.ascontiguousarray(np.asarray(a, dtype=np.float32))
    if "nc" not in _CACHE:
        _CACHE["nc"] = build()
        _CACHE["cst"] = make_consts()
    nc = _CACHE["nc"]
    x_prompt = f(x_prompt); x_sample = f(x_sample); state_C = f(state_C); state_n = f(state_n); state_m = f(state_m)
    cache_k = f(cache_k); cache_v = f(cache_v); c_prompt = f(c_prompt); c_sample = f(c_sample)
    shared = {
        "w_ada": f(w_ada)[0], "b_ada": f(b_ada), "g_norm": f(g_norm), "w_in": f(w_in)[0], "b_i": f(b_igate), "b_f": f(b_fgate),
        "g_mn": f(g_mnorm), "sinks": f(sinks), "w_mo": f(w_m_out)[0], "w_ao": f(w_a_out)[0], "w_o": f(w_out)[0],
        "g_final": f(g_final).reshape(1, D), "cst": _CACHE["cst"],
    }
    in_maps = []
    for c in range(NCORES):
        b0, b1 = 16 * c, 16 * (c + 1)
        m = dict(shared)
        m["xp"] = x_prompt[c]
        m["xs"] = x_sample[b0:b1].reshape(64, D)
        m["cc"] = np.concatenate([c_prompt[c:c + 1], c_sample[b0:b1]], axis=0)
        m["stC"] = state_C[0, b0:b1]
        m["stn"] = state_n[0, b0:b1].reshape(64, 128)
        m["stm"] = state_m[0, b0:b1]
        m["ck"] = cache_k[0, b0:b1].reshape(16, 128, 128)
        m["cv"] = cache_v[0, b0:b1].reshape(16, 128, 128)
        in_maps.append(m)
    res = run_bass_kernel_spmd(nc, in_maps, core_ids=list(range(NCORES)), **({"trace": True} if TRACE else {}))
    if TRACE:
        _CACHE["exec_ns"] = res.exec_time_ns
    r = res.results
    if DEBUG:
        _CACHE["dbg"] = {k: r[0][k] for k in DBG_NAMES}
    cat = lambda k: np.stack([r[c][k] for c in range(NCORES)], axis=0)
    y_prompt = cat("yp")
    y_sample = cat("ys").reshape(128, 4, D)
    C_p = cat("Cp")[None]
    n_p = cat("np")[None]
    m_p = cat("mp").reshape(1, 8, 4)
    k_p = cat("kp").reshape(1, 8, 128, 2, 64)
    v_p = cat("vp").reshape(1, 8, 128, 2, 64)
    C_s = cat("Cs").reshape(1, 128, 4, 128, 128)
    n_s = cat("ns").reshape(1, 128, 4, 128)
    m_s = cat("ms").reshape(1, 128, 4)
    k_s = cat("ks").reshape(1, 128, 128, 2, 64)
    v_s = cat("vs").reshape(1, 128, 128, 2, 64)
    return (y_prompt, y_sample, C_p, n_p, m_p, k_p, v_p, C_s, n_s, m_s, k_s, v_s)
```
